# Optimizing a Trainium2 kernel written in Bass

```python
import math
import jax, jax.numpy as jnp
from jax import lax
import numpy as np

D_MODEL = 1024
BATCH = 2
SEQ = 8192
DEPTH = 4
DEC_BATCH = 128
DEC_SEQ = 4
PAST_LEN = 8192
PAGE_SIZE = 128

N_A_LAYERS = DEPTH // 2
N_B_LAYERS = DEPTH - N_A_LAYERS
GROUP_SIZE = 16
N_GROUPS = D_MODEL // GROUP_SIZE
STATE_DIM = 64
DT_MIN = 0.001
DT_MAX = 0.1
HEAD_DIM = 64
N_HEADS = D_MODEL // HEAD_DIM
N_KV_HEADS = max(1, N_HEADS // 8)
Q_PER_KV = N_HEADS // N_KV_HEADS
WINDOW = 128
BLOCK = WINDOW
ATTN_SCALE = 1.0 / math.sqrt(HEAD_DIM)
NUM_BUCKETS = 32
MAX_DISTANCE = WINDOW
D_FF = ((8 * D_MODEL // 3 + 127) // 128) * 128
N_NORMS = 6
RMS_EPS = 1e-6

kernel_name = 'yoco_s5_swa_sink_macaron'


def _rms(x, g):
    xf = x.astype(jnp.float32)
    y = xf * lax.rsqrt(jnp.mean(xf * xf, axis=-1, keepdims=True) + RMS_EPS) * g.astype(jnp.float32)
    return y.astype(x.dtype)


def _swiglu(x, w_gu, w_down):
    gate, up = jnp.split(x @ w_gu, 2, axis=-1)
    return (jax.nn.silu(gate) * up) @ w_down


def _ssm_combine(left, right):
    a_l, b_l = left
    a_r, b_r = right
    return a_r * a_l, a_r * b_l + b_r


def _ssm_mixer(u, lam_re, lam_im, log_dt, b_re, b_im, c_re, c_im, d, w_glu, b_glu, x0_re, x0_im):
    f32 = jnp.float32
    bsz, seq, _ = u.shape
    uf = u.astype(f32)
    ug = uf.reshape(bsz, seq, N_GROUPS, GROUP_SIZE)
    lam = lax.complex(lam_re.astype(f32), lam_im.astype(f32))
    dt = jnp.exp(log_dt.astype(f32))
    lam_dt = lam * dt
    lam_bar = jnp.exp(lam_dt)
    b_bar = ((lam_bar - 1.0) / lam)[..., None] * lax.complex(b_re.astype(f32), b_im.astype(f32))
    bu = lax.complex(jnp.einsum('bsgh,gph->bsgp', ug, jnp.real(b_bar)),
                     jnp.einsum('bsgh,gph->bsgp', ug, jnp.imag(b_bar)))
    a = jnp.broadcast_to(lam_bar, bu.shape)
    xs = lax.associative_scan(_ssm_combine, (a, bu), axis=1)[1]
    if x0_re is not None:
        t = jnp.arange(1, seq + 1, dtype=f32)[:, None, None]
        x0 = lax.complex(x0_re.astype(f32), x0_im.astype(f32))
        xs = xs + jnp.exp(lam_dt * t) * x0[:, None]
    y = (jnp.einsum('bsgp,ghp->bsgh', jnp.real(xs), c_re.astype(f32))
         - jnp.einsum('bsgp,ghp->bsgh', jnp.imag(xs), c_im.astype(f32)))
    y = y.reshape(bsz, seq, D_MODEL) + d.astype(f32) * uf
    h = jax.nn.gelu(y) @ w_glu.astype(f32) + b_glu.astype(f32)
    out = h[..., :D_MODEL] * jax.nn.sigmoid(h[..., D_MODEL:])
    last = xs[:, -1]
    return out.astype(u.dtype), jnp.real(last), jnp.imag(last)


def _t5_bucket(dist):
    n = jnp.maximum(dist, 0)
    max_exact = NUM_BUCKETS // 2
    nf = jnp.maximum(n, 1).astype(jnp.float32)
    large = max_exact + (jnp.log(nf / max_exact) / math.log(MAX_DISTANCE / max_exact)
                         * (NUM_BUCKETS - max_exact)).astype(jnp.int32)
    large = jnp.minimum(large, NUM_BUCKETS - 1)
    return jnp.where(n < max_exact, n, large)


def _band_bias_mask(rel_bias, n_q, n_k, q_offset):
    dist = (jnp.arange(n_q)[:, None] + q_offset) - jnp.arange(n_k)[None, :]
    valid = (dist >= 0) & (dist < WINDOW)
    bias = rel_bias.astype(jnp.float32)[_t5_bucket(dist)]
    bias = bias.transpose(2, 0, 1).reshape(N_KV_HEADS, Q_PER_KV, n_q, n_k)
    return bias, valid


def _prompt_bands(k, v, rel_bias):
    bsz, seq = k.shape[:2]
    nb = seq // BLOCK

    def band(t):
        tb = t.reshape(bsz, nb, BLOCK, N_KV_HEADS, HEAD_DIM)
        prev = jnp.concatenate([jnp.zeros_like(tb[:, :1]), tb[:, :-1]], axis=1)
        return jnp.concatenate([prev, tb], axis=2)

    bias, valid = _band_bias_mask(rel_bias, BLOCK, 2 * BLOCK, BLOCK)
    key_ok = (jnp.arange(nb)[:, None] > 0) | (jnp.arange(2 * BLOCK)[None, :] >= BLOCK)
    return band(k), band(v), bias, valid[None] & key_ok[:, None, :]


def _window_attention(u, k_cat, v_cat, bias, mask, w_q, b_q, sinks, w_o, b_o):
    f32 = jnp.float32
    bsz, seq, _ = u.shape
    n_blk = k_cat.shape[1]
    q = (u @ w_q + b_q).reshape(bsz, n_blk, seq // n_blk, N_KV_HEADS, Q_PER_KV, HEAD_DIM)
    s = jnp.einsum('bnqkgd,bnjkd->bnkgqj', q, k_cat, preferred_element_type=f32) * ATTN_SCALE + bias
    s = jnp.where(mask[None, :, None, None], s, -jnp.inf)
    sink = sinks.astype(f32).reshape(N_KV_HEADS, Q_PER_KV)[None, None, :, :, None, None]
    mx = jnp.maximum(s.max(axis=-1, keepdims=True), sink)
    p = jnp.exp(s - mx)
    p = p / (p.sum(axis=-1, keepdims=True) + jnp.exp(sink - mx))
    o = jnp.einsum('bnkgqj,bnjkd->bnqkgd', p.astype(v_cat.dtype), v_cat, preferred_element_type=f32)
    return o.reshape(bsz, seq, N_HEADS * HEAD_DIM).astype(u.dtype) @ w_o + b_o


def _trunk(x, ssm_re0, ssm_im0, win_k0, win_v0, w):
    bsz, seq, _ = x.shape
    new_re, new_im = [], []
    for l in range(DEPTH):
        if l == N_A_LAYERS:
            kv = (_rms(x, w['kv_norm_g']) @ w['w_kv'] + w['b_kv']).reshape(bsz, seq, 2, N_KV_HEADS, HEAD_DIM)
            k, v = kv[:, :, 0], kv[:, :, 1]
            if win_k0 is None:
                k_cat, v_cat, bias, mask = _prompt_bands(k, v, w['rel_bias'])
                new_k, new_v = k[:, -WINDOW:], v[:, -WINDOW:]
            else:
                n_past = win_k0.shape[1]
                k_full = jnp.concatenate([win_k0.astype(k.dtype), k], axis=1)
                v_full = jnp.concatenate([win_v0.astype(v.dtype), v], axis=1)
                bias, valid = _band_bias_mask(w['rel_bias'], seq, n_past + seq, n_past)
                k_cat, v_cat, mask = k_full[:, None], v_full[:, None], valid[None]
                new_k, new_v = k_full[:, -WINDOW:], v_full[:, -WINDOW:]
        g = w['norm_g'][l]
        x = x + 0.5 * _rms(_swiglu(_rms(x, g[0]), w['ffn1_w_gu'][l], w['ffn1_w_down'][l]), g[1])
        u = _rms(x, g[2])
        if l < N_A_LAYERS:
            m, fr, fi = _ssm_mixer(
                u, w['ssm_lambda_re'][l], w['ssm_lambda_im'][l], w['ssm_log_dt'][l],
                w['ssm_b_re'][l], w['ssm_b_im'][l], w['ssm_c_re'][l], w['ssm_c_im'][l],
                w['ssm_d'][l], w['ssm_w_glu'][l], w['ssm_b_glu'][l],
                None if ssm_re0 is None else ssm_re0[l],
                None if ssm_im0 is None else ssm_im0[l])
            new_re.append(fr)
            new_im.append(fi)
        else:
            bl = l - N_A_LAYERS
            m = _window_attention(u, k_cat, v_cat, bias, mask, w['attn_w_q'][bl], w['attn_b_q'][bl],
                                  w['attn_sinks'][bl], w['attn_w_o'][bl], w['attn_b_o'][bl])
        x = x + _rms(m, g[3])
        x = x + 0.5 * _rms(_swiglu(_rms(x, g[4]), w['ffn2_w_gu'][l], w['ffn2_w_down'][l]), g[5])
    return x, jnp.stack(new_re), jnp.stack(new_im), new_k, new_v


def setup_inputs(seed: int = 0) -> dict:
    key = jax.random.key(seed)
    kit = iter(list(jax.random.split(key, 64)))
    f32 = jnp.float32

    def nrm(shape, scale):
        return jax.random.normal(next(kit), shape, f32) * scale

    win_rows = min(WINDOW, PAST_LEN)
    kvw = 2 * N_KV_HEADS * HEAD_DIM
    qw = N_HEADS * HEAD_DIM
    return {
        'x_prompt': nrm((BATCH, SEQ, D_MODEL), 1.0),
        'x_sample': nrm((DEC_BATCH, DEC_SEQ, D_MODEL), 1.0),
        'state_ssm_re': nrm((N_A_LAYERS, DEC_BATCH, N_GROUPS, STATE_DIM), 0.5),
        'state_ssm_im': nrm((N_A_LAYERS, DEC_BATCH, N_GROUPS, STATE_DIM), 0.5),
        'cache_win_k': nrm((DEC_BATCH, win_rows, N_KV_HEADS, HEAD_DIM), 1.0),
        'cache_win_v': nrm((DEC_BATCH, win_rows, N_KV_HEADS, HEAD_DIM), 1.0),
        'norm_g': 1.0 + nrm((DEPTH, N_NORMS, D_MODEL), 0.05),
        'ffn1_w_gu': nrm((DEPTH, D_MODEL, 2 * D_FF), D_MODEL ** -0.5),
        'ffn1_w_down': nrm((DEPTH, D_FF, D_MODEL), D_FF ** -0.5),
        'ffn2_w_gu': nrm((DEPTH, D_MODEL, 2 * D_FF), D_MODEL ** -0.5),
        'ffn2_w_down': nrm((DEPTH, D_FF, D_MODEL), D_FF ** -0.5),
        'ssm_lambda_re': -0.5 + nrm((N_A_LAYERS, N_GROUPS, STATE_DIM), 0.01),
        'ssm_lambda_im': jnp.pi * jnp.arange(STATE_DIM, dtype=f32) + nrm((N_A_LAYERS, N_GROUPS, STATE_DIM), 0.01),
        'ssm_log_dt': jax.random.uniform(next(kit), (N_A_LAYERS, N_GROUPS, STATE_DIM), f32,
                                         math.log(DT_MIN), math.log(DT_MAX)),
        'ssm_b_re': nrm((N_A_LAYERS, N_GROUPS, STATE_DIM, GROUP_SIZE), (2 * GROUP_SIZE) ** -0.5),
        'ssm_b_im': nrm((N_A_LAYERS, N_GROUPS, STATE_DIM, GROUP_SIZE), (2 * GROUP_SIZE) ** -0.5),
        'ssm_c_re': nrm((N_A_LAYERS, N_GROUPS, GROUP_SIZE, STATE_DIM), STATE_DIM ** -0.5),
        'ssm_c_im': nrm((N_A_LAYERS, N_GROUPS, GROUP_SIZE, STATE_DIM), STATE_DIM ** -0.5),
        'ssm_d': nrm((N_A_LAYERS, D_MODEL), 1.0),
        'ssm_w_glu': nrm((N_A_LAYERS, D_MODEL, 2 * D_MODEL), D_MODEL ** -0.5),
        'ssm_b_glu': nrm((N_A_LAYERS, 2 * D_MODEL), 0.01),
        'kv_norm_g': 1.0 + nrm((D_MODEL,), 0.05),
        'w_kv': nrm((D_MODEL, kvw), D_MODEL ** -0.5),
        'b_kv': nrm((kvw,), 0.01),
        'attn_w_q': nrm((N_B_LAYERS, D_MODEL, qw), D_MODEL ** -0.5),
        'attn_b_q': nrm((N_B_LAYERS, qw), 0.01),
        'attn_sinks': nrm((N_B_LAYERS, N_HEADS), 0.5),
        'attn_w_o': nrm((N_B_LAYERS, qw, D_MODEL), qw ** -0.5),
        'attn_b_o': nrm((N_B_LAYERS, D_MODEL), 0.01),
        'rel_bias': nrm((NUM_BUCKETS, N_HEADS), 0.5),
    }


def reference(x_prompt, x_sample, state_ssm_re, state_ssm_im, cache_win_k, cache_win_v,
              norm_g, ffn1_w_gu, ffn1_w_down, ffn2_w_gu, ffn2_w_down,
              ssm_lambda_re, ssm_lambda_im, ssm_log_dt, ssm_b_re, ssm_b_im, ssm_c_re, ssm_c_im,
              ssm_d, ssm_w_glu, ssm_b_glu,
              kv_norm_g, w_kv, b_kv,
              attn_w_q, attn_b_q, attn_sinks, attn_w_o, attn_b_o, rel_bias):
    w = dict(norm_g=norm_g, ffn1_w_gu=ffn1_w_gu, ffn1_w_down=ffn1_w_down,
             ffn2_w_gu=ffn2_w_gu, ffn2_w_down=ffn2_w_down,
             ssm_lambda_re=ssm_lambda_re, ssm_lambda_im=ssm_lambda_im, ssm_log_dt=ssm_log_dt,
             ssm_b_re=ssm_b_re, ssm_b_im=ssm_b_im, ssm_c_re=ssm_c_re, ssm_c_im=ssm_c_im,
             ssm_d=ssm_d, ssm_w_glu=ssm_w_glu, ssm_b_glu=ssm_b_glu,
             kv_norm_g=kv_norm_g, w_kv=w_kv, b_kv=b_kv,
             attn_w_q=attn_w_q, attn_b_q=attn_b_q, attn_sinks=attn_sinks,
             attn_w_o=attn_w_o, attn_b_o=attn_b_o, rel_bias=rel_bias)
    y_prompt, re_p, im_p, k_p, v_p = _trunk(x_prompt, None, None, None, None, w)
    y_sample, re_s, im_s, k_s, v_s = _trunk(x_sample, state_ssm_re, state_ssm_im, cache_win_k, cache_win_v, w)
    return (y_prompt, y_sample, re_p, im_p, k_p, v_p, re_s, im_s, k_s, v_s)
```

```python
import math
from contextlib import ExitStack
import numpy as np
import concourse.bass as bass
import concourse.mybir as mybir
from concourse.bass_utils import run_bass_kernel_spmd

F32 = mybir.dt.float32
BF16 = mybir.dt.bfloat16
I32 = mybir.dt.int32
AF = mybir.ActivationFunctionType
ALU = mybir.AluOpType
AX = mybir.AxisListType

NCORES = 8
D = 1024
DFF = 2816
DEPTH = 4
NA = 2
KC = 8
FC = 22
TP = 2048
TS = 64
T = TP + TS
SB = 16
TILES = [(0, 512), (512, 1024), (1024, 1536), (1536, 1792), (1792, 2112)]
GROUPS = [[0], [1], [2], [3, 4]]
GMAX = 576
EPS = 1e-6
NEG = -30000.0
ATTN_SCALE = 0.125
ENGS = ('pe', 'act', 'dve', 'pool', 'sp')


class Buf:
    __slots__ = ('name', 'w', 'r')

    def __init__(self, name):
        self.name = name
        self.w = None
        self.r = {}


class Prog:
    def __init__(self, nc, stack):
        self.nc = nc
        self.stack = stack
        self.q = {k: [] for k in ENGS}
        self.cnt = {}
        self.semh = {}
        self.known = {k: {} for k in ENGS}
        self.out_toks = []
        for k in ENGS:
            self._sem('e_' + k)
        self.ndsem = 0
        self.dsem_of = {}

    def _sem(self, key):
        if key not in self.semh:
            self.semh[key] = self.stack.enter_context(self.nc.semaphore(key))
            self.cnt[key] = 0
        return key

    def dsem(self, buf):
        k = self.dsem_of.get(id(buf))
        if k is None:
            k = self._sem('d%d' % self.ndsem)
            self.ndsem += 1
            self.dsem_of[id(buf)] = k
        return k

    def op(self, eng, fn, reads=(), writes=(), dma=None, out=False, ss=False):
        deps = {}

        def add(tok):
            if tok is not None and deps.get(tok[0], 0) < tok[1]:
                deps[tok[0]] = tok[1]
        for b in reads:
            add(b.w)
        for b in writes:
            add(b.w)
            for s, v in b.r.items():
                add((s, v))
        own = 'e_' + eng
        kn = self.known[eng]
        waits = []
        for s, v in deps.items():
            if dma is None and s == own:
                if not (ss and self.cnt[own] - v <= 1):
                    continue
            if kn.get(s, 0) >= v:
                continue
            kn[s] = v
            waits.append((s, v))
        if dma is None:
            s, inc = own, 1
        else:
            s, inc = self.dsem(dma), 16
        self.cnt[s] += inc
        tok = (s, self.cnt[s])
        for b in reads:
            if b.r.get(s, 0) < tok[1]:
                b.r[s] = tok[1]
        for b in writes:
            b.w = tok
            b.r = {}
        self.q[eng].append((waits, fn, s, inc))
        if out:
            self.out_toks.append(tok)
        return tok

    def finish(self):
        final = {}
        for s, v in self.out_toks:
            final[s] = max(final.get(s, 0), v)
        for k in ENGS:
            if k != 'sp':
                final['e_' + k] = self.cnt['e_' + k]
        waits = [(s, v) for s, v in final.items() if v > 0]
        self.q['sp'].append((waits, None, None, 0))

    def replay(self):
        nc = self.nc
        semh = self.semh
        q = self.q
        with nc.Block() as block:
            def mk(name):
                def body(e):
                    for waits, fn, s, inc in q[name]:
                        for ws, wv in waits:
                            e.wait_ge(semh[ws], wv)
                        if fn is not None:
                            ins = fn(e)
                            ins.then_inc(semh[s], inc)
                return body
            block.tensor(mk('pe'))
            block.scalar(mk('act'))
            block.vector(mk('dve'))
            block.gpsimd(mk('pool'))
            block.sync(mk('sp'))


class Rot:
    def __init__(self, items):
        self.items = items
        self.i = 0

    def next(self):
        it = self.items[self.i % len(self.items)]
        self.i += 1
        return it


def build(cfg):
    nc = bass.Bass("TRN2", target_bir_lowering=False)

    def din(name, shape, dt=F32):
        return nc.dram_tensor(name, list(shape), dt, kind="ExternalInput").ap()

    def dout(name, shape, dt=F32):
        return nc.dram_tensor(name, list(shape), dt, kind="ExternalOutput").ap()

    def dint(name, shape, dt=F32):
        return nc.dram_tensor(name, list(shape), dt, kind="Internal").ap()

    xT = din("xT", [D, T])
    nlw = DEPTH * 2 if ('ffn1' in cfg['phases'] or 'ffn2' in cfg['phases']) else 1
    wgu = din("wgu", [nlw, FC, 128, KC, 256])
    wdn = din("wdn", [nlw, KC, 128, FC, 128])
    gall = din("gall", [128, 25, KC])
    yT = dout("yT", [D, T])
    lamT = din("lamT", [NA, 128, 3, 32])
    lamF = din("lamF", [NA, 3, 4096])
    BTd = din("BTd", [NA, 2, 128, 4096])
    CPd = din("CPd", [NA, 2, 128, 4096])
    dvec = din("dvec", [128, NA, KC])
    bglu = din("bglu", [128, NA, 16])
    wglu = din("wglu", [NA, KC, 128, KC, 256])
    s0d = din("s0d", [NA, 128, 2, 32, SB])
    selS = din("selS", [128, 3, 8])
    jvec = din("jvec", [128, 128])
    st_p = dout("st_p", [NA, 128, 2, 32])
    st_s = dout("st_s", [NA, 128, 2, 32, SB])
    W13d = dint("W13d", [NA, KC, 128, 16, 128], BF16)
    wq = din("wq", [2, 4, 128, KC, 256])
    wo = din("wo", [2, 4, 128, KC, 256])
    wkv = din("wkv", [128, KC, 256])
    bq = din("bq", [128, 2, KC])
    bo = din("bo", [128, 2, KC])
    bk = din("bk", [128, 1])
    bkv_row = din("bkv_row", [128, 256])
    sinks = din("sinks", [128, 2, 16])
    sinks_s = din("sinks_s", [32, 2, 2])
    selKV = din("selKV", [128, 8])
    ident = din("ident", [128, 128])
    biasT = din("biasT", [128, 16, 256])
    maskT = din("maskT", [128, 256])
    bias_s = din("bias_s", [32, 2, 132])
    mask_s = din("mask_s", [32, 132])
    blk0mask = din("blk0mask", [128, 256])
    ck = din("ck", [SB, 128, 128])
    cv = din("cv", [SB, 128, 128])
    kv_last = dout("kv_last", [128, 256])
    ks_out = dout("ks_out", [SB, 128, 128])
    vs_out = dout("vs_out", [SB, 128, 128])
    gin_kv = dint("gin_kv", [128, 256])
    gout_kv = dint("gout_kv", [NCORES * 128, 256])
    vn_scr = dint("vn_scr", [4, SB, 128])
    dbg_sv = dout("dbg_sv", [4, 128, 1024])
    dbg_ta = dout("dbg_ta", [4, 128, 8192])
    dbg_w = dout("dbg_w", [KC, 128, 16, 128], BF16)
    gin = [dint("gin%d" % l, [128, 64]) for l in range(NA)]
    gout = [dint("gout%d" % l, [NCORES * 128, 64]) for l in range(NA)]

    with ExitStack() as stack:
        P = Prog(nc, stack)

        def sb(name, shape, dt=F32):
            return stack.enter_context(nc.sbuf_tensor(name, list(shape), dt))

        def bc_last(ap, n):
            return bass.AP(ap.tensor, ap.offset, [list(x) for x in ap.ap] + [[0, n]])

        def bc_col(ap, n):
            return bass.AP(ap.tensor, ap.offset, [list(ap.ap[0]), [0, n]])

        def bc_mid(ap, m):
            a = [list(x) for x in ap.ap]
            return bass.AP(ap.tensor, ap.offset, [a[0], [0, m]] + a[1:])

        X = sb("X", [128, KC, T])
        YTb = sb("YTb", [128, KC * GMAX])
        YT = YTb[:].rearrange("p (c t) -> p c t", c=KC)
        XN = YTb[:, 0:KC * GMAX // 2].bitcast(BF16).rearrange("p (c t) -> p c t", c=KC)
        HB = sb("HB", [128, FC * GMAX], BF16)
        H = HB[:].rearrange("p (f t) -> p f t", f=FC)
        HF = HB[:].bitcast(F32)
        HI = HB[:].bitcast(I32)
        WG = [sb("WG%d" % i, [128, KC, 256], BF16) for i in range(3)]
        WD = [sb("WD%d" % i, [128, FC, 128], BF16) for i in range(2)]
        G = sb("G", [128, 25, KC])
        GH = sb("GH", [128, 25, KC])
        SQ = [sb("SQ%d" % i, [128, 512], BF16) for i in range(3)]
        RS = [sb("RS%d" % i, [128, 512]) for i in range(2)]
        SG = [sb("SG%d" % i, [128, 512]) for i in range(2)]
        TMP = [sb("TMP%d" % i, [128, 512]) for i in range(2)]
        ONES = sb("ONES", [128, 128], BF16)
        EPSC = sb("EPSC", [128, 1])
        PS = stack.enter_context(nc.psum_tensor("PS", [128, 8, 512], F32))
        PERS = sb("PERS", [128, 9216])
        MIX = sb("MIX", [128, 4224])
        DV = sb("DV", [128, NA, KC])
        DG = sb("DG", [128, NA, KC])
        BGL = sb("BGL", [128, NA, 16])
        SELS = sb("SELS", [128, 3, 8])
        ITMP = sb("ITMP", [128, 512], I32)
        ITS = sb("ITS", [128, 32], I32)
        TA = PERS[:, 0:4096].rearrange("p (q j) -> p q j", q=32)
        TB = PERS[:, 4096:8192].rearrange("p (q j) -> p q j", q=32)
        RS2 = PERS[:, 8192:8192 + GMAX]
        SV = MIX[:, 0:1024].rearrange("p (k q) -> p k q", k=32)
        SC = [MIX[:, 1024 + 128 * i:1024 + 128 * (i + 1)] for i in range(8)]
        XF = [MIX[:, 2048 + 256 * i:2048 + 256 * (i + 1)].rearrange("p (c j) -> p c j", c=2) for i in range(2)]
        ACC = MIX[:, 2560:3072].rearrange("p (k b q) -> p k b q", k=4, b=4)
        S0 = MIX[:, 3072:4096].rearrange("p (c q b) -> p c q b", c=2, q=32)
        JV = MIX[:, 4096:4224]
        YGB = H[:, 0:8, :]
        XSL = [H[:, 8 + 2 * i:10 + 2 * i, :] for i in range(6)]

        NT = len(TILES)
        bX = [Buf("X%d" % i) for i in range(NT)]
        bY = Buf("YTXN")
        bXN = [bY, bY]
        bYT = [bY, bY]
        bH = [Buf("H%d" % i) for i in range(2)]
        rWG = Rot([(WG[i], Buf("WG%d" % i)) for i in range(3)])
        rWD = Rot([(WD[i], Buf("WD%d" % i)) for i in range(2)])
        bPS = [Buf("PS%d" % i) for i in range(8)]
        rPS = Rot([(PS[:, i, :], bPS[i]) for i in range(5)])
        rSQ = Rot([(SQ[i], Buf("SQ%d" % i)) for i in range(3)])
        rRS = Rot([(RS[i], Buf("RS%d" % i)) for i in range(2)])
        rSG = Rot([(SG[i], Buf("SG%d" % i)) for i in range(2)])
        rTMP = Rot([(TMP[i], Buf("TMP%d" % i)) for i in range(2)])
        bG = Buf("G")
        bC = Buf("consts")
        bS5 = Buf("s5small")
        bTab = Buf("tables")
        bRS2 = Buf("RS2")
        bYGB = Buf("YGB")
        rXS = Rot([(XSL[i], Buf("XS%d" % i)) for i in range(6)])
        rXF = Rot([(XF[i], Buf("XF%d" % i)) for i in range(2)])
        bSC = Buf("SC")
        bACC = Buf("ACC")
        bS0 = Buf("S0")
        bW13 = [Buf("W13d%d" % l) for l in range(NA)]
        bGin = [Buf("gin%d" % l) for l in range(NA)]
        bGout = [Buf("gout%d" % l) for l in range(NA)]

        def slots(g):
            res = []
            off = 0
            for ti in GROUPS[g]:
                lo, hi = TILES[ti]
                res.append((ti, lo, hi, off, hi - lo))
                off += hi - lo
            return res

        P.op('dve', lambda e: e.memset(ONES[:], 1.0), writes=[bC])
        P.op('dve', lambda e: e.memset(EPSC[:], EPS), writes=[bC])
        P.op('sp', lambda e: e.dma_start(out=G[:], in_=gall), writes=[bG], dma=bG)
        P.op('dve', lambda e: e.tensor_scalar(out=GH[:], in0=G[:], scalar1=0.5, scalar2=None, op0=ALU.mult),
             reads=[bG], writes=[bC])
        bSm = Buf("smallin")
        for dst, src in ((DV, dvec), (BGL, bglu), (SELS, selS)):
            P.op('sp', (lambda e, dst=dst, src=src: e.dma_start(out=dst[:], in_=src)), writes=[bSm], dma=bSm)
        xTv = xT.rearrange("(c p) t -> p c t", p=128)
        yTv = yT.rearrange("(c p) t -> p c t", p=128)
        for ti, (lo, hi) in enumerate(TILES):
            for c in range(KC):
                P.op('sp', (lambda e, c=c, lo=lo, hi=hi: e.dma_start(out=X[:, c, lo:hi], in_=xTv[:, c, lo:hi])),
                     writes=[bX[ti]], dma=bX[ti])

        def finish_stats(ps, bps, n):
            rs, brs = rRS.next()
            P.op('act', (lambda e: e.activation(out=rs[:, :n], in_=ps[:, :n], func=AF.Sqrt,
                                                bias=EPSC[:], scale=1.0 / D)),
                 reads=[bps, bC], writes=[brs])
            P.op('dve', (lambda e: e.reciprocal(out=rs[:, :n], in_=rs[:, :n])), reads=[brs], writes=[brs])
            return rs, brs

        def rms_stats(src_fn, n, src_bufs):
            ps, bps = rPS.next()
            for c in range(KC):
                sq, bsq = rSQ.next()
                P.op('act', (lambda e, c=c, sq=sq: e.activation(out=sq[:, :n], in_=src_fn(c), func=AF.Square)),
                     reads=src_bufs, writes=[bsq])
                P.op('pe', (lambda e, c=c, sq=sq, ps=ps: e.matmul(ps[:, :n], lhsT=ONES[:], rhs=sq[:, :n],
                                                                 start=(c == 0), stop=(c == KC - 1))),
                     reads=[bsq, bC], writes=[bps])
            return finish_stats(ps, bps, n)

        def prenorm(g, gidx, keep_rs=False):
            for si, (ti, lo, hi, off, n) in enumerate(slots(g)):
                rs, brs = rms_stats(lambda c, lo=lo, hi=hi: X[:, c, lo:hi], n, [bX[ti]])
                for c in range(KC):
                    P.op('dve', (lambda e, c=c, lo=lo, hi=hi, off=off, n=n, rs=rs: e.scalar_tensor_tensor(
                        out=XN[:, c, off:off + n], in0=X[:, c, lo:hi], scalar=G[:, gidx, c:c + 1],
                        in1=rs[:, :n], op0=ALU.mult, op1=ALU.mult)),
                        reads=[bX[ti], brs, bG], writes=[bXN[si]])
                if keep_rs:
                    P.op('act', (lambda e, off=off, n=n, rs=rs: e.activation(out=RS2[:, off:off + n], in_=rs[:, :n],
                                                                            func=AF.Copy)),
                         reads=[brs], writes=[bRS2])

        def postnorm(g, gidx, half, stats):
            GG = GH if half else G
            for si, (ti, lo, hi, off, n) in enumerate(slots(g)):
                ps, bps = stats[si]
                rs, brs = finish_stats(ps, bps, n)
                for c in range(KC):
                    tmp, btmp = rTMP.next()
                    P.op('dve', (lambda e, c=c, off=off, n=n, rs=rs, tmp=tmp: e.scalar_tensor_tensor(
                        out=tmp[:, :n], in0=YT[:, c, off:off + n], scalar=GG[:, gidx, c:c + 1],
                        in1=rs[:, :n], op0=ALU.mult, op1=ALU.mult)),
                        reads=[bYT[si], brs, bG, bC], writes=[btmp])
                    P.op('dve', (lambda e, c=c, lo=lo, hi=hi, n=n, tmp=tmp: e.tensor_tensor(
                        out=X[:, c, lo:hi], in0=X[:, c, lo:hi], in1=tmp[:, :n], op=ALU.add)),
                        reads=[btmp], writes=[bX[ti]])

        def load_w(eng, dst, bdst, src, extra_reads=()):
            P.op(eng, (lambda e: e.dma_start(out=dst, in_=src, max_dma_last_dim=4096)),
                 reads=list(extra_reads), writes=[bdst], dma=bdst)

        def stat_accum(si, oc, noc, off, n, stats):
            sq, bsq = rSQ.next()
            P.op('act', (lambda e: e.activation(out=sq[:, :n], in_=YT[:, oc, off:off + n], func=AF.Square)),
                 reads=[bYT[si]], writes=[bsq])
            sps, bsps = stats[si]
            P.op('pe', (lambda e: e.matmul(sps[:, :n], lhsT=ONES[:], rhs=sq[:, :n], start=(oc == 0),
                                           stop=(oc == noc - 1))),
                 reads=[bsq, bC], writes=[bsps])

        def ffn(l, which, g):
            li = l * 2 + which
            prenorm(g, l * 6 + (0 if which == 0 else 4))
            sl = slots(g)
            for j in range(FC):
                wb, bwb = rWG.next()
                load_w('pool', wb[:], bwb, wgu[li, j])
                for si, (ti, lo, hi, off, n) in enumerate(sl):
                    psg, bpsg = rPS.next()
                    psu, bpsu = rPS.next()

                    def mm(e, wb=wb, psg=psg, psu=psu, off=off, n=n):
                        ins = None
                        for half, ps in ((0, psg), (1, psu)):
                            for kc in range(KC):
                                ins = e.matmul(ps[:, :n], lhsT=wb[:, kc, half * 128:(half + 1) * 128],
                                               rhs=XN[:, kc, off:off + n], start=(kc == 0), stop=(kc == KC - 1))
                        return ins
                    P.op('pe', mm, reads=[bwb, bXN[si]], writes=[bpsg, bpsu])
                    sg, bsg = rSG.next()
                    P.op('act', (lambda e, sg=sg, psg=psg, n=n: e.activation(out=sg[:, :n], in_=psg[:, :n],
                                                                            func=AF.Silu)),
                         reads=[bpsg], writes=[bsg])
                    P.op('dve', (lambda e, sg=sg, psu=psu, j=j, off=off, n=n: e.tensor_tensor(
                        out=H[:, j, off:off + n], in0=sg[:, :n], in1=psu[:, :n], op=ALU.mult)),
                        reads=[bsg, bpsu], writes=[bH[si]])
            stats = [(PS[:, 6 + si, :], bPS[6 + si]) for si in range(len(sl))]
            for oc in range(KC):
                wb, bwb = rWD.next()
                load_w('pool', wb[:], bwb, wdn[li, oc])
                for si, (ti, lo, hi, off, n) in enumerate(sl):
                    ps, bps = rPS.next()

                    def mm(e, wb=wb, ps=ps, off=off, n=n):
                        ins = None
                        for fc in range(FC):
                            ins = e.matmul(ps[:, :n], lhsT=wb[:, fc, :], rhs=H[:, fc, off:off + n],
                                           start=(fc == 0), stop=(fc == FC - 1))
                        return ins
                    P.op('pe', mm, reads=[bwb, bH[si]], writes=[bps])
                    P.op('act', (lambda e, oc=oc, ps=ps, off=off, n=n: e.activation(
                        out=YT[:, oc, off:off + n], in_=ps[:, :n], func=AF.Copy)),
                        reads=[bps], writes=[bYT[si]])
                    stat_accum(si, oc, KC, off, n, stats)
            postnorm(g, l * 6 + (1 if which == 0 else 5), True, stats)
        TWO_PI = 2.0 * math.pi
        SIN_SCALE = TWO_PI * (1.0 - 2e-6)
        hz = [bH[0], bH[1]]

        def trig(cyc, sin_out, cos_out, t1, t1i, bufs):
            rw = dict(reads=bufs, writes=bufs, ss=True)
            for _ in range(6):
                P.op('dve', lambda e: e.tensor_scalar(out=t1, in0=cyc, scalar1=0.5, scalar2=None, op0=ALU.is_gt), **rw)
                P.op('dve', lambda e: e.tensor_tensor(out=cyc, in0=cyc, in1=t1, op=ALU.subtract), **rw)
            P.op('act', lambda e: e.activation(out=sin_out, in_=cyc, func=AF.Sin, scale=SIN_SCALE), **rw)
            P.op('dve', lambda e: e.tensor_scalar(out=cyc, in0=cyc, scalar1=0.25, scalar2=None, op0=ALU.add), **rw)
            P.op('dve', lambda e: e.tensor_scalar(out=t1, in0=cyc, scalar1=0.5, scalar2=None, op0=ALU.is_gt), **rw)
            P.op('dve', lambda e: e.tensor_tensor(out=cyc, in0=cyc, in1=t1, op=ALU.subtract), **rw)
            P.op('act', lambda e: e.activation(out=cos_out, in_=cyc, func=AF.Sin, scale=SIN_SCALE), **rw)

        def tt(out, a, b, op, bufs, eng='dve'):
            P.op(eng, lambda e: e.tensor_tensor(out=out, in0=a, in1=b, op=op), reads=bufs, writes=bufs, ss=True)

        def ts(out, a, s1, op0, bufs, s2=None, op1=None):
            if op1 is None:
                P.op('dve', lambda e: e.tensor_scalar(out=out, in0=a, scalar1=s1, scalar2=None, op0=op0),
                     reads=bufs, writes=bufs, ss=True)
            else:
                P.op('dve', lambda e: e.tensor_scalar(out=out, in0=a, scalar1=s1, scalar2=s2, op0=op0, op1=op1),
                     reads=bufs, writes=bufs, ss=True)

        def cmul(ore, oim, are_, aim_, bre_, bim_, t1, t2, bufs):
            tt(t1, are_, bre_, ALU.mult, bufs)
            tt(t2, aim_, bim_, ALU.mult, bufs)
            tt(ore, t1, t2, ALU.subtract, bufs)
            tt(t1, are_, bim_, ALU.mult, bufs)
            tt(t2, aim_, bre_, ALU.mult, bufs)
            tt(oim, t1, t2, ALU.add, bufs)

        LRE, LIM, LDT, DT_, ARE, TH, RDEC = 0, 1, 2, 3, 4, 5, 6
        A1R, A1I, A128R, A128I = 7, 8, 9, 10
        AKR = [11, 13, 15]
        AKI = [12, 14, 16]
        ER, EI, CR, CI = 17, 18, 19, 20
        T0 = 21
        NA1I = 29
        SVI = ITS[:]

        def sv(i):
            return SV[:, i, :]

        UR, UI, PWR, PWI = 27, 28, 30, 31

        def csq(bufs):
            tt(sv(T0), sv(PWR), sv(PWR), ALU.mult, bufs)
            tt(sv(T0 + 1), sv(PWI), sv(PWI), ALU.mult, bufs)
            tt(sv(T0 + 2), sv(PWR), sv(PWI), ALU.mult, bufs)
            tt(sv(PWR), sv(T0), sv(T0 + 1), ALU.subtract, bufs)
            ts(sv(PWI), sv(T0 + 2), 2.0, ALU.mult, bufs)

        def s5_small(l):
            bufs = [bS5, bSC]
            P.op('sp', lambda e: e.dma_start(out=SV[:, 0:3, :], in_=lamT[l]), writes=bufs, dma=bS5)
            P.op('sp', lambda e: e.dma_start(out=JV, in_=jvec), writes=bufs, dma=bS5)
            P.op('sp', lambda e: e.dma_start(out=S0, in_=s0d[l]), writes=[bS0], dma=bS0)
            P.op('act', lambda e: e.activation(out=sv(DT_), in_=sv(LDT), func=AF.Exp), reads=bufs, writes=bufs)
            tt(sv(ARE), sv(LRE), sv(DT_), ALU.mult, bufs)
            tt(sv(TH), sv(LIM), sv(DT_), ALU.mult, bufs)
            P.op('act', lambda e: e.activation(out=sv(RDEC), in_=sv(ARE), func=AF.Exp), reads=bufs, writes=bufs)
            ts(sv(T0), sv(TH), 1.0 / TWO_PI, ALU.mult, bufs)
            trig(sv(T0), sv(UI), sv(UR), sv(T0 + 1), None, bufs)
            tt(sv(A1R), sv(RDEC), sv(UR), ALU.mult, bufs)
            tt(sv(A1I), sv(RDEC), sv(UI), ALU.mult, bufs)
            ts(sv(NA1I), sv(A1I), -1.0, ALU.mult, bufs)
            P.op('dve', lambda e: e.memset(SV[:, ER:EI + 1, :], 0.0), reads=bufs, writes=bufs)
            P.op('dve', lambda e: e.tensor_tensor(out=DG[:, l, :], in0=DV[:, l, :], in1=G[:, l * 6 + 2, :],
                                                  op=ALU.mult), reads=[bSm, bG], writes=[bC])

        def s5_tables(mode):
            bufs = [bS5, bSC, bTab] + hz
            X1 = HF[:, 0:2048].rearrange("p (q j) -> p q j", q=32)
            X2 = HF[:, 2048:4096].rearrange("p (q j) -> p q j", q=32)
            if mode == 'G':
                tt(sv(PWR), sv(A1R), sv(A1R), ALU.max, bufs)
                tt(sv(PWI), sv(A1I), sv(A1I), ALU.max, bufs)
                P.op('dve', lambda e: e.memset(TA[:, :, 127:128], 1.0), reads=bufs, writes=bufs)
                P.op('dve', lambda e: e.memset(TB[:, :, 127:128], 0.0), reads=bufs, writes=bufs)
            else:
                tt(sv(PWR), sv(UR), sv(UR), ALU.max, bufs)
                tt(sv(PWI), sv(UI), sv(UI), ALU.max, bufs)
                P.op('dve', lambda e: e.tensor_copy(out=TA[:, :, 0:1], in_=SV[:, UR, :].unsqueeze(2)),
                     reads=bufs, writes=bufs)
                P.op('dve', lambda e: e.tensor_copy(out=TB[:, :, 0:1], in_=SV[:, UI, :].unsqueeze(2)),
                     reads=bufs, writes=bufs)
            n = 1
            while n < 128:
                if mode == 'G':
                    src = slice(128 - n, 128)
                    dst = slice(128 - 2 * n, 128 - n)
                else:
                    src = slice(0, n)
                    dst = slice(n, 2 * n)
                pr = bc_last(SV[:, PWR, :], n)
                pi = bc_last(SV[:, PWI, :], n)
                x1, x2 = X1[:, :, 0:n], X2[:, :, 0:n]
                tt(x1, TA[:, :, src], pr, ALU.mult, bufs)
                tt(x2, TB[:, :, src], pi, ALU.mult, bufs)
                tt(TA[:, :, dst], x1, x2, ALU.subtract, bufs)
                tt(x1, TA[:, :, src], pi, ALU.mult, bufs)
                tt(x2, TB[:, :, src], pr, ALU.mult, bufs)
                tt(TB[:, :, dst], x1, x2, ALU.add, bufs)
                csq(bufs)
                n *= 2
            if mode == 'G':
                tt(sv(A128R), sv(PWR), sv(PWR), ALU.max, bufs)
                tt(sv(A128I), sv(PWI), sv(PWI), ALU.max, bufs)
                for _ in range(4):
                    csq(bufs)
                tt(sv(AKR[0]), sv(PWR), sv(PWR), ALU.max, bufs)
                tt(sv(AKI[0]), sv(PWI), sv(PWI), ALU.max, bufs)
                csq(bufs)
                tt(sv(AKR[1]), sv(PWR), sv(PWR), ALU.max, bufs)
                tt(sv(AKI[1]), sv(PWI), sv(PWI), ALU.max, bufs)
                cmul(sv(AKR[2]), sv(AKI[2]), sv(AKR[0]), sv(AKI[0]), sv(AKR[1]), sv(AKI[1]), sv(T0), sv(T0 + 1), bufs)

        def s5_weights(l):
            btm = [rTMP.items[0][1], rTMP.items[1][1]]
            bufs = [bS5, bSC, bTab] + hz + [bY] + btm
            F = [HF[:, 512 * i:512 * (i + 1)] for i in range(12)]
            FI = ITMP[:]
            Y = [YTb[:, 512 * i:512 * (i + 1)] for i in range(9)]
            OB = [TMP[0][:].bitcast(BF16), TMP[1][:].bitcast(BF16)]
            lre, lim, ldt, dt_, are_, th_, sn, cs, nr, ni, t1 = F[0:11]
            w1r, w1i, bre, bim, t2, t3, den, cr, ci = Y[0:9]
            for ch in range(KC):
                cols = slice(ch * 512, ch * 512 + 512)
                for k, dst in enumerate((lre, lim, ldt)):
                    src = lamF[l, k, cols]
                    srcb = bass.AP(src.tensor, src.offset, [[0, 128], [1, 512]])
                    P.op('sp', lambda e, dst=dst, srcb=srcb: e.dma_start(out=dst, in_=srcb), writes=bufs, dma=bS5)
                for k, dst in enumerate((bre, bim)):
                    P.op('sp', lambda e, dst=dst, k=k, cols=cols: e.dma_start(out=dst, in_=BTd[l, k, :, cols]),
                         writes=bufs, dma=bS5)
                P.op('act', lambda e: e.activation(out=dt_, in_=ldt, func=AF.Exp), reads=bufs, writes=bufs)
                tt(are_, lre, dt_, ALU.mult, bufs)
                tt(th_, lim, dt_, ALU.mult, bufs)
                ts(th_, th_, 1.0 / TWO_PI, ALU.mult, bufs)
                trig(th_, sn, cs, t1, None, bufs)
                mag = dt_
                P.op('act', lambda e: e.activation(out=mag, in_=are_, func=AF.Exp), reads=bufs, writes=bufs)
                tt(nr, mag, cs, ALU.mult, bufs)
                ts(nr, nr, -1.0, ALU.add, bufs)
                tt(ni, mag, sn, ALU.mult, bufs)
                tt(den, lre, lre, ALU.mult, bufs)
                tt(t1, lim, lim, ALU.mult, bufs)
                tt(den, den, t1, ALU.add, bufs)
                P.op('dve', lambda e: e.reciprocal(out=den, in_=den), reads=bufs, writes=bufs)
                tt(t1, nr, lre, ALU.mult, bufs)
                tt(t2, ni, lim, ALU.mult, bufs)
                tt(cr, t1, t2, ALU.add, bufs)
                tt(cr, cr, den, ALU.mult, bufs)
                tt(t1, ni, lre, ALU.mult, bufs)
                tt(t2, nr, lim, ALU.mult, bufs)
                tt(ci, t1, t2, ALU.subtract, bufs)
                tt(ci, ci, den, ALU.mult, bufs)
                cmul(w1r, w1i, bre, bim, cr, ci, t2, t3, bufs)
                obv = OB[0].rearrange("p (r c j) -> p r c j", r=4, c=2)
                P.op('act', lambda e, obv=obv: e.activation(
                    out=obv[:, :, 0, :], in_=w1r.rearrange("p (r j) -> p r j", r=4), func=AF.Copy),
                    reads=bufs, writes=bufs)
                P.op('act', lambda e, obv=obv: e.activation(
                    out=obv[:, :, 1, :], in_=w1i.rearrange("p (r j) -> p r j", r=4), func=AF.Copy),
                    reads=bufs, writes=bufs)
                P.op('sp', lambda e, ch=ch: e.dma_start(
                    out=W13d[l, ch, :, 0:8, :], in_=OB[0].rearrange("p (k j) -> p k j", k=8)),
                    reads=bufs, writes=[bW13[l]], dma=bW13[l])
                cre, cim = F[0], F[1]
                for k, dst in enumerate((cre, cim)):
                    P.op('sp', lambda e, dst=dst, k=k, cols=cols: e.dma_start(out=dst, in_=CPd[l, k, :, cols]),
                         writes=bufs, dma=bS5)
                obv2 = OB[1].rearrange("p (r c j) -> p r c j", r=4, c=2)
                P.op('act', lambda e, obv2=obv2: e.activation(
                    out=obv2[:, :, 0, :], in_=cre.rearrange("p (r j) -> p r j", r=4), func=AF.Copy),
                    reads=bufs, writes=bufs)
                P.op('act', lambda e, obv2=obv2: e.activation(
                    out=obv2[:, :, 1, :], in_=cim.rearrange("p (r j) -> p r j", r=4), func=AF.Copy, scale=-1.0),
                    reads=bufs, writes=bufs)
                P.op('sp', lambda e, ch=ch: e.dma_start(
                    out=W13d[l, ch, :, 8:16, :], in_=OB[1].rearrange("p (k j) -> p k j", k=8)),
                    reads=bufs, writes=[bW13[l]], dma=bW13[l])
        def bu_matmul(wb, bwb, r, off, n, ch):
            pr, bpr = rPS.next()
            pi, bpi = rPS.next()

            def mm(e):
                e.matmul(pr[:, :n], lhsT=wb[:, r * 2 + 0, :], rhs=XN[:, ch, off:off + n], start=True, stop=True)
                return e.matmul(pi[:, :n], lhsT=wb[:, r * 2 + 1, :], rhs=XN[:, ch, off:off + n],
                                start=True, stop=True)
            P.op('pe', mm, reads=[bwb, bY], writes=[bpr, bpi])
            return pr, bpr, pi, bpi

        def p1_pair(ch, r, wb, bwb):
            q = ch * 4 + r
            pr, bpr, pi, bpi = bu_matmul(wb, bwb, r, 0, 512, ch)
            for blk in range(4):
                js = slice(blk * 128, blk * 128 + 128)
                for kind, (src, tab) in enumerate(((pr, TA), (pi, TB), (pr, TB), (pi, TA))):
                    P.op('dve', (lambda e, src=src, tab=tab, kind=kind, blk=blk, js=js:
                                 e.scalar_tensor_tensor(out=SC[0], in0=src[:, js], scalar=1.0,
                                                        in1=tab[:, q, :], op0=ALU.mult, op1=ALU.mult,
                                                        accum_out=ACC[:, kind, blk, q:q + 1])),
                         reads=[bpr, bpi, bTab, bSC], writes=[bSC, bACC])

        def load_w13(l, ch):
            wb3, bwb = rWG.next()
            wb = wb3[:].rearrange("p a (b j) -> p (a b) j", b=2)
            load_w('pool', wb, bwb, W13d[l, ch], extra_reads=[bW13[l]])
            return wb, bwb

        def s5_pass1(l):
            s5_tables('G')
            for g in range(len(GROUPS)):
                prenorm(g, l * 6 + 2)
                P.op('dve', lambda e: e.memset(ACC[:], 0.0), reads=[bACC], writes=[bACC])
                for ch in range(KC):
                    wb, bwb = load_w13(l, ch)
                    for r in range(4):
                        p1_pair(ch, r, wb, bwb)
                bufs = [bS5, bSC, bACC]
                for blk in range(4):
                    tt(sv(T0), ACC[:, 0, blk, :], ACC[:, 1, blk, :], ALU.subtract, bufs)
                    tt(sv(T0 + 1), ACC[:, 2, blk, :], ACC[:, 3, blk, :], ALU.add, bufs)
                    cmul(sv(T0 + 2), sv(T0 + 3), sv(A128R), sv(A128I), sv(ER), sv(EI), sv(T0 + 4), sv(T0 + 5), bufs)
                    tt(sv(ER), sv(T0 + 2), sv(T0), ALU.add, bufs)
                    tt(sv(EI), sv(T0 + 3), sv(T0 + 1), ALU.add, bufs)

        def s5_exchange(l):
            bufs = [bS5, bSC]
            btm = [rTMP.items[0][1], rTMP.items[1][1]]
            EG = TMP[0][:].rearrange("p (j c) -> p j c", j=8)
            EGT = TMP[1][:].rearrange("p (j c) -> p j c", j=8)
            P.op('sp', lambda e: e.dma_start(out=gin[l].rearrange("p (c q) -> p c q", c=2), in_=SV[:, ER:EI + 1, :]),
                 reads=bufs, writes=[bGin[l]], dma=bGin[l])
            P.op('pool', lambda e: e.collective_compute("AllGather", ALU.bypass,
                                                        replica_groups=[list(range(NCORES))],
                                                        ins=[gin[l]], outs=[gout[l]]),
                 reads=[bGin[l]], writes=[bGout[l]])
            P.op('sp', lambda e: e.dma_start(out=EG, in_=gout[l].rearrange("(j p) c -> p j c", p=128)),
                 reads=[bGout[l]], writes=[btm[0]], dma=btm[0])
            allb = bufs + btm + [bSm]
            P.op('dve', lambda e: e.memset(SV[:, CR:CI + 1, :], 0.0), reads=allb, writes=allb)
            for k in range(3):
                tt(EGT, EG, bc_last(SELS[:, k, :], 64), ALU.mult, allb)
                P.op('dve', lambda e: e.tensor_reduce(out=SV[:, T0:T0 + 2, :].rearrange("p a q -> p (a q)"),
                                                      in_=EGT.rearrange("p j c -> p c j"), axis=AX.X, op=ALU.add),
                     reads=allb, writes=allb, ss=True)
                if k == 0:
                    tt(sv(CR), sv(CR), sv(T0), ALU.add, allb)
                    tt(sv(CI), sv(CI), sv(T0 + 1), ALU.add, allb)
                else:
                    cmul(sv(T0 + 2), sv(T0 + 3), sv(AKR[k - 1]), sv(AKI[k - 1]), sv(T0), sv(T0 + 1), sv(T0 + 4),
                         sv(T0 + 5), allb)
                    tt(sv(CR), sv(CR), sv(T0 + 2), ALU.add, allb)
                    tt(sv(CI), sv(CI), sv(T0 + 3), ALU.add, allb)
            cmul(sv(T0 + 2), sv(T0 + 3), sv(AKR[0]), sv(AKI[0]), sv(CR), sv(CI), sv(T0 + 4), sv(T0 + 5), allb)
            tt(sv(ER), sv(ER), sv(T0 + 2), ALU.add, allb)
            tt(sv(EI), sv(EI), sv(T0 + 3), ALU.add, allb)
            P.op('sp', lambda e: e.dma_start(out=st_p[l], in_=SV[:, ER:EI + 1, :]), reads=allb, dma=bS5, out=True)

        def p2_block(q, blk, pr, bpr, pi, bpi, xs, bxs):
            js = slice(blk * 128, blk * 128 + 128)
            ct, st = TA[:, q, :], TB[:, q, :]
            rd = [bpr, bpi, bTab, bSC, bS5]
            ta_, tb_, tc_, td_ = SC[0], SC[1], SC[2], SC[3]
            m_re, m_im, z_re, z_im = SC[4], SC[5], SC[6], SC[7]

            def o(fn, reads=rd, writes=(bSC,)):
                P.op('dve', fn, reads=reads, writes=list(writes))
            o(lambda e: e.tensor_tensor(out=ta_, in0=pr[:, js], in1=ct, op=ALU.mult))
            o(lambda e: e.tensor_tensor(out=tb_, in0=pi[:, js], in1=st, op=ALU.mult))
            o(lambda e: e.tensor_tensor(out=tc_, in0=pi[:, js], in1=ct, op=ALU.mult))
            o(lambda e: e.tensor_tensor(out=td_, in0=pr[:, js], in1=st, op=ALU.mult))
            o(lambda e: e.tensor_tensor(out=m_re, in0=ta_, in1=tb_, op=ALU.add))
            o(lambda e: e.tensor_tensor(out=m_im, in0=tc_, in1=td_, op=ALU.subtract))
            rdec = bc_col(SV[:, RDEC, q:q + 1], 128)
            o(lambda e: e.tensor_tensor_scan(out=z_re, data0=rdec, data1=m_re, initial=SV[:, CR, q:q + 1],
                                             op0=ALU.mult, op1=ALU.add))
            o(lambda e: e.tensor_tensor_scan(out=z_im, data0=rdec, data1=m_im, initial=SV[:, CI, q:q + 1],
                                             op0=ALU.mult, op1=ALU.add))
            xf, bxf = rXF.next()
            o(lambda e: e.tensor_tensor(out=ta_, in0=z_re, in1=ct, op=ALU.mult))
            o(lambda e: e.tensor_tensor(out=tb_, in0=z_im, in1=st, op=ALU.mult))
            o(lambda e: e.tensor_tensor(out=tc_, in0=z_re, in1=st, op=ALU.mult))
            o(lambda e: e.tensor_tensor(out=td_, in0=z_im, in1=ct, op=ALU.mult))
            o(lambda e: e.tensor_tensor(out=xf[:, 0, :], in0=ta_, in1=tb_, op=ALU.subtract), writes=(bSC, bxf))
            o(lambda e: e.tensor_tensor(out=xf[:, 1, :], in0=tc_, in1=td_, op=ALU.add), writes=(bSC, bxf))
            P.op('dve', lambda e: e.tensor_copy(out=SV[:, CR:CI + 1, q], in_=xf[:, :, 127]), reads=[bxf, bS5],
                 writes=[bS5], ss=True)
            P.op('act', lambda e: e.activation(out=xs[:, :, js], in_=xf[:, :, :], func=AF.Copy),
                 reads=[bxf], writes=[bxs])

        def p2_sample(ch, r, wb, bwb, xs, bxs):
            q = ch * 4 + r
            ps_, bps_ = rPS.next()

            def mm(e):
                e.matmul(ps_[:, 0:TS], lhsT=wb[:, r * 2 + 0, :], rhs=XN[:, ch, 512:512 + TS], start=True, stop=True)
                return e.matmul(ps_[:, TS:2 * TS], lhsT=wb[:, r * 2 + 1, :], rhs=XN[:, ch, 512:512 + TS],
                                start=True, stop=True)
            P.op('pe', mm, reads=[bwb, bY], writes=[bps_])
            bsr = ps_[:, 0:TS].rearrange("p (b s) -> p b s", s=4)
            bsi = ps_[:, TS:2 * TS].rearrange("p (b s) -> p b s", s=4)
            rd = [bps_, bS0, bS5, bSC]
            sr, si_ = S0[:, 0, q, :], S0[:, 1, q, :]
            tq = SC[0][:, 32:48]
            nr_, ni_ = SC[0][:, 0:SB], SC[0][:, SB:2 * SB]
            xsv = xs[:, :, 512:512 + TS].rearrange("p c (b s) -> p c b s", s=4)

            def step(s_):
                P.op('dve', lambda e: e.scalar_tensor_tensor(out=tq, in0=sr, scalar=SV[:, A1R, q:q + 1],
                                                             in1=bsr[:, :, s_], op0=ALU.mult, op1=ALU.add),
                     reads=rd, writes=[bSC], ss=True)
                P.op('dve', lambda e: e.scalar_tensor_tensor(out=nr_, in0=si_, scalar=SV[:, NA1I, q:q + 1], in1=tq,
                                                             op0=ALU.mult, op1=ALU.add), reads=rd, writes=[bSC], ss=True)
                P.op('dve', lambda e: e.scalar_tensor_tensor(out=tq, in0=si_, scalar=SV[:, A1R, q:q + 1],
                                                             in1=bsi[:, :, s_], op0=ALU.mult, op1=ALU.add),
                     reads=rd, writes=[bSC], ss=True)
                P.op('dve', lambda e: e.scalar_tensor_tensor(out=ni_, in0=sr, scalar=SV[:, A1I, q:q + 1], in1=tq,
                                                             op0=ALU.mult, op1=ALU.add), reads=rd, writes=[bSC], ss=True)
                P.op('dve', lambda e: e.tensor_copy(out=S0[:, :, q, :],
                                                    in_=SC[0][:, 0:2 * SB].rearrange("p (c b) -> p c b", c=2)),
                     reads=[bSC], writes=[bS0], ss=True)
                P.op('act', lambda e: e.activation(out=xsv[:, :, :, s_], in_=S0[:, :, q, :], func=AF.Copy),
                     reads=[bS0], writes=[bxs])
            for s_ in range(4):
                step(s_)

        def p2_pair(ch, r, wb, bwb, psy, bpsy, has_s):
            q = ch * 4 + r
            pr, bpr, pi, bpi = bu_matmul(wb, bwb, r, 0, 512, ch)
            xs, bxs = rXS.next()
            for blk in range(4):
                p2_block(q, blk, pr, bpr, pi, bpi, xs, bxs)
            if has_s:
                p2_sample(ch, r, wb, bwb, xs, bxs)

            def mmc(e):
                e.matmul(psy[:, 0:512], lhsT=wb[:, 8 + r * 2, :], rhs=xs[:, 0, 0:512], start=(r == 0), stop=False)
                return e.matmul(psy[:, 0:512], lhsT=wb[:, 8 + r * 2 + 1, :], rhs=xs[:, 1, 0:512],
                                start=False, stop=(r == 3))
            P.op('pe', mmc, reads=[bwb, bxs], writes=[bpsy])
            return xs, bxs

        def p2_yseg(l, g, ch, py, bpy, off, n):
            if off == 0:
                lo = TILES[GROUPS[g][0]][0]
                xb = [bX[ti] for ti in GROUPS[g]]
            else:
                lo = TP
                xb = [bX[GROUPS[g][-1]]]
            tmp, btmp = rTMP.next()
            P.op('dve', lambda e: e.scalar_tensor_tensor(out=tmp[:, :n], in0=X[:, ch, lo:lo + n],
                                                         scalar=DG[:, l, ch:ch + 1], in1=RS2[:, off:off + n],
                                                         op0=ALU.mult, op1=ALU.mult),
                 reads=xb + [bRS2, bC], writes=[btmp])
            P.op('dve', lambda e: e.tensor_tensor(out=tmp[:, :n], in0=tmp[:, :n], in1=py[:, :n], op=ALU.add),
                 reads=[bpy], writes=[btmp])
            P.op('act', lambda e: e.activation(out=YGB[:, ch, off:off + n], in_=tmp[:, :n], func=AF.Gelu_apprx_tanh),
                 reads=[btmp], writes=[bYGB])

        def p2_chunk(l, g, ch, pb, has_s):
            wb, bwb = load_w13(l, ch)
            psy, bpsy = PS[:, pb, :], bPS[pb]
            xs_list = [p2_pair(ch, r, wb, bwb, psy, bpsy, has_s) for r in range(4)]
            p2_yseg(l, g, ch, psy, bpsy, 0, 512)
            if has_s:
                pys, bpys = rPS.next()

                def mms(e):
                    ins = None
                    for r, (xs, _) in enumerate(xs_list):
                        for c in range(2):
                            ins = e.matmul(pys[:, 0:TS], lhsT=wb[:, 8 + r * 2 + c, :], rhs=xs[:, c, 512:512 + TS],
                                           start=(r == 0 and c == 0), stop=(r == 3 and c == 1))
                    return ins
                P.op('pe', mms, reads=[bwb] + [b for _, b in xs_list], writes=[bpys])
                p2_yseg(l, g, ch, pys, bpys, 512, TS)

        def glu_block(l, j, sl, stats):
            wbg, bwbg = rWG.next()
            load_w('pool', wbg[:], bwbg, wglu[l, j])

            def one(si, off, n):
                psv, bpsv = rPS.next()
                psg, bpsg = rPS.next()

                def mm(e):
                    ins = None
                    for half, ps in ((0, psv), (1, psg)):
                        for kc in range(KC):
                            ins = e.matmul(ps[:, :n], lhsT=wbg[:, kc, half * 128:(half + 1) * 128],
                                           rhs=YGB[:, kc, off:off + n], start=(kc == 0), stop=(kc == KC - 1))
                    return ins
                P.op('pe', mm, reads=[bwbg, bYGB], writes=[bpsv, bpsg])
                sg, bsg = rSG.next()
                P.op('act', lambda e: e.activation(out=sg[:, :n], in_=psg[:, :n], func=AF.Sigmoid,
                                                   bias=BGL[:, l, 8 + j:9 + j]),
                     reads=[bpsg, bSm], writes=[bsg])
                P.op('dve', lambda e: e.scalar_tensor_tensor(out=YT[:, j, off:off + n], in0=psv[:, :n],
                                                             scalar=BGL[:, l, j:j + 1], in1=sg[:, :n],
                                                             op0=ALU.add, op1=ALU.mult),
                     reads=[bpsv, bsg, bSm], writes=[bYT[si]])
                stat_accum(si, j, KC, off, n, stats)
            for si, (ti, lo, hi, off, n) in enumerate(sl):
                one(si, off, n)

        def s5_pass2(l):
            s5_tables('CS')
            psy_banks = [5, 7]
            nchunk = 0
            for g in range(len(GROUPS)):
                has_s = (g == len(GROUPS) - 1)
                prenorm(g, l * 6 + 2, keep_rs=True)
                for ch in range(KC):
                    p2_chunk(l, g, ch, psy_banks[nchunk % 2], has_s)
                    nchunk += 1
                sl = slots(g)
                stats = [(PS[:, 6 + si, :], bPS[6 + si]) for si in range(len(sl))]
                for j in range(KC):
                    glu_block(l, j, sl, stats)
                postnorm(g, l * 6 + 3, False, stats)
            P.op('sp', lambda e: e.dma_start(out=st_s[l], in_=S0), reads=[bS0], dma=bS0, out=True)

        dbgn = [0]

        def dump(tag):
            i = dbgn[0]
            dbgn[0] += 1
            if i >= 4:
                return
            bd = Buf("dbg%d" % i)
            allb = [bS5, bSC, bTab, bACC]
            P.op('sp', lambda e: e.dma_start(out=dbg_sv[i], in_=MIX[:, 0:1024]), reads=allb, dma=bd, out=True)
            P.op('sp', lambda e: e.dma_start(out=dbg_ta[i], in_=PERS[:, 0:8192]), reads=allb, dma=bd, out=True)

        def s5_layer(l):
            stop = cfg.get('stop', 99)
            s5_small(l)
            dump('small')
            s5_weights(l)
            if l == 0:
                bd = Buf("dbgw")
                P.op('sp', lambda e: e.dma_start(out=dbg_w, in_=W13d[0]), reads=[bW13[0]], dma=bd, out=True)
            if stop <= 1:
                return
            s5_pass1(l)
            dump('pass1')
            if stop <= 2:
                return
            s5_exchange(l)
            dump('exch')
            if stop <= 3:
                return
            s5_pass2(l)
            dump('pass2')
        BIAS = PERS[:, 0:4096].rearrange("p (h j) -> p h j", h=16)
        PB = PERS[:, 4096:8320].bitcast(BF16)
        KFM = PB[:, 0:2176]
        VTM = PB[:, 2176:4352].rearrange("p (b c) -> p b c", b=17)
        KCF = PB[:, 4352:6400].rearrange("p (b w) -> p b w", b=SB)
        VC = PB[:, 6400:8448].rearrange("p (b c) -> p b c", b=SB)
        KNF = PERS[:, 8320:8352].bitcast(BF16)
        BSs = PERS[0:32, 8352:8616].rearrange("p (k j) -> p k j", k=2)
        B0M = PERS[:, 8616:8872]
        QFM = H[:, 0:8, :]
        OFM = H[:, 8:16, :]
        bKV = Buf("KV")
        bBias = Buf("bias")
        bQ = Buf("Qfm")
        bO = Buf("Ofm")
        bAT = Buf("attn_tmp")
        AS = [MIX[:, 256 * i:256 * (i + 1)] for i in range(2)]
        AE = [MIX[:, 512 + 256 * i:512 + 256 * (i + 1)] for i in range(2)]
        APN = [MIX[:, 1024 + 128 * i:1024 + 128 * (i + 1)].bitcast(BF16) for i in range(2)]
        APT = [MIX[:, 1280 + 128 * i:1280 + 128 * (i + 1)].bitcast(BF16) for i in range(2)]
        AST = [MIX[:, 1536 + 8 * i:1536 + 8 * (i + 1)] for i in range(4)]
        rAS = Rot([(AS[i], Buf("AS%d" % i)) for i in range(2)])
        rAE = Rot([(AE[i], Buf("AE%d" % i)) for i in range(2)])
        rAPN = Rot([(APN[i], Buf("APN%d" % i)) for i in range(2)])
        rAPT = Rot([(APT[i], Buf("APT%d" % i)) for i in range(2)])
        rAST = Rot([(AST[i], Buf("AST%d" % i)) for i in range(4)])
        VN4 = MIX[0:4, 2048:3072].bitcast(BF16).rearrange("p (b c) -> p b c", b=SB)
        SSs = MIX[0:32, 3072:3204]
        SEs = MIX[0:32, 3204:3336]
        SPN = MIX[0:32, 3336:3402].bitcast(BF16)
        SPT = MIX[:, 3402:3418].bitcast(BF16)
        SPT4 = MIX[0:4, 3418:3434].bitcast(BF16)
        SST_ = MIX[0:32, 3434:3442]
        QS = MIX[:, 3442:3698].bitcast(BF16).rearrange("p (b c s) -> p b c s", b=SB, c=KC)
        IDF = sb("IDF", [128, 128])
        IDB = sb("IDB", [128, 128], BF16)
        BQ8 = sb("BQ8", [128, 2, KC])
        BO = sb("BO", [128, 2, KC])
        BK = sb("BK", [128, 1])
        SNK = sb("SNK", [128, 2, 16])
        SNKS = sb("SNKS", [32, 2, 2])
        SELK = sb("SELK", [128, 8])
        BKVR = sb("BKVR", [128, 256])
        bGinKV = Buf("ginkv")
        bGoutKV = Buf("goutkv")
        bVnScr = Buf("vnscr")
        bAtC = Buf("attn_consts")

        def attn_consts():
            for dst, src in ((IDF, ident), (BQ8, bq), (BO, bo), (BK, bk), (SNK, sinks), (SNKS, sinks_s),
                             (SELK, selKV), (BKVR, bkv_row)):
                P.op('sp', (lambda e, dst=dst, src=src: e.dma_start(out=dst[:], in_=src)), writes=[bAtC], dma=bAtC)
            P.op('dve', lambda e: e.tensor_copy(out=IDB[:], in_=IDF[:]), reads=[bAtC], writes=[bAtC])
            P.op('dve', lambda e: e.tensor_scalar(out=BQ8[:], in0=BQ8[:], scalar1=ATTN_SCALE, scalar2=None,
                                                  op0=ALU.mult), reads=[bAtC], writes=[bAtC])

        def kv_phase():
            bufs_t = [bS5, bSC, bTab, bACC, bS0]
            tm0, btm0 = rTMP.items[0]
            P.op('sp', lambda e: e.dma_start(out=BIAS, in_=biasT), reads=bufs_t, writes=[bBias, bTab], dma=bBias)
            P.op('sp', lambda e: e.dma_start(out=tm0[:, 0:256], in_=maskT), writes=[btm0], dma=btm0)
            P.op('dve', lambda e: e.tensor_tensor(out=BIAS, in0=BIAS, in1=bc_mid(tm0[:, 0:256], 16), op=ALU.add),
                 reads=[bBias, btm0], writes=[bBias])
            P.op('sp', lambda e: e.dma_start(out=BSs, in_=bias_s), reads=bufs_t, writes=[bBias], dma=bBias)
            P.op('sp', lambda e: e.dma_start(out=tm0[0:32, 256:388], in_=mask_s), writes=[btm0], dma=btm0)
            P.op('dve', lambda e: e.tensor_tensor(out=BSs, in0=BSs, in1=bc_mid(tm0[0:32, 256:388], 2), op=ALU.add),
                 reads=[bBias, btm0], writes=[bBias])
            P.op('sp', lambda e: e.dma_start(out=B0M, in_=blk0mask), reads=bufs_t, writes=[bBias], dma=bBias)
            KCN = HB[:, 0:2048].rearrange("p (b c) -> p b c", b=SB)
            bKCN = Buf("KCN")
            P.op('pool', lambda e: e.dma_start(out=KCN, in_=ck.rearrange("b w c -> w b c"), max_dma_last_dim=4096),
                 reads=hz + [bYGB], writes=[bKCN] + hz, dma=bKCN)
            P.op('pool', lambda e: e.dma_start(out=VC, in_=cv.rearrange("b w c -> w b c"), max_dma_last_dim=4096),
                 reads=bufs_t, writes=[bKV], dma=bKV)
            for b in range(SB):
                kv_ktr(b, KCN, bKCN)
            bdd = Buf("dd")
            P.op('sp', lambda e: e.dma_start(out=ks_out[:, 0:124, :], in_=ck[:, 4:128, :]), dma=bdd, out=True)
            P.op('sp', lambda e: e.dma_start(out=vs_out[:, 0:124, :], in_=cv[:, 4:128, :]), dma=bdd, out=True)
            for g in range(len(GROUPS)):
                prenorm(g, 24)
                wb, bwb = rWG.next()
                load_w('pool', wb[:], bwb, wkv)
                for si, (ti, lo, hi, off, n) in enumerate(slots(g)):
                    kv_kfm(wb, bwb, lo, hi, off, n)
                    for t0 in range(lo, hi, 128):
                        kv_tm(wb, bwb, t0, off + (t0 - lo), min(128, hi - t0))
            kv_halo()

        def kv_ktr(b, KCN, bKCN):
            ps, bps = rPS.next()
            psb = ps[:, 0:64].bitcast(BF16)
            P.op('pe', lambda e: e.transpose(out=psb, in_=KCN[:, b, :], identity=IDB[:]), reads=[bKCN, bAtC],
                 writes=[bps])
            P.op('act', lambda e: e.activation(out=KCF[:, b, :], in_=psb, func=AF.Copy), reads=[bps], writes=[bKV])

        def kv_kfm(wb, bwb, lo, hi, off, n):
            ps, bps = rPS.next()

            def mm(e):
                ins = None
                for kc in range(KC):
                    ins = e.matmul(ps[:, :n], lhsT=wb[:, kc, 0:128], rhs=XN[:, kc, off:off + n],
                                   start=(kc == 0), stop=(kc == KC - 1))
                return ins
            P.op('pe', mm, reads=[bwb, bY], writes=[bps])
            npr = min(hi, TP) - lo
            if npr > 0:
                P.op('act', lambda e: e.activation(out=KFM[:, 128 + lo:128 + lo + npr], in_=ps[:, 0:npr],
                                                   func=AF.Identity, bias=BK[:]), reads=[bps, bAtC], writes=[bKV])
            if hi > TP:
                P.op('act', lambda e: e.activation(out=KNF[:, 0:TS], in_=ps[:, npr:npr + TS], func=AF.Identity,
                                                   bias=BK[:]), reads=[bps, bAtC], writes=[bKV])

        def kv_tm(wb, bwb, t0, xoff, nt):
            ps, bps = rPS.next()

            def mm(e):
                ins = None
                for kc in range(KC):
                    ins = e.matmul(ps[0:nt, 0:256], lhsT=XN[:, kc, xoff:xoff + nt], rhs=wb[:, kc, :],
                                   start=(kc == 0), stop=(kc == KC - 1))
                return ins
            P.op('pe', mm, reads=[bwb, bY], writes=[bps])
            tm, btm = rTMP.next()
            P.op('dve', lambda e: e.tensor_tensor(out=tm[0:nt, 0:256], in0=ps[0:nt, 0:256], in1=BKVR[0:nt, :],
                                                  op=ALU.add), reads=[bps, bAtC], writes=[btm])
            if t0 < TP:
                blk = t0 // 128
                P.op('act', lambda e: e.activation(out=VTM[:, blk + 1, :], in_=tm[:, 128:256], func=AF.Copy),
                     reads=[btm], writes=[bKV])
                if blk == 15:
                    P.op('sp', lambda e: e.dma_start(out=kv_last, in_=tm[:, 0:256]), reads=[btm], dma=btm, out=True)
                    P.op('sp', lambda e: e.dma_start(out=gin_kv, in_=tm[:, 0:256]), reads=[btm], writes=[bGinKV],
                         dma=bGinKV)
            else:
                for s_ in range(4):
                    a_k = tm[s_:s_ + 1, 0:128]
                    a_v = tm[s_:s_ + 1, 128:256]
                    pst = tm[:].ap[0][0]
                    src_k = bass.AP(a_k.tensor, a_k.offset, [[4 * pst, SB], [1, 128]])
                    src_v = bass.AP(a_v.tensor, a_v.offset, [[4 * pst, SB], [1, 128]])
                    P.op('sp', lambda e, s_=s_, src_k=src_k: e.dma_start(out=ks_out[:, 124 + s_, :], in_=src_k),
                         reads=[btm], dma=btm, out=True)
                    P.op('sp', lambda e, s_=s_, src_v=src_v: e.dma_start(out=vs_out[:, 124 + s_, :], in_=src_v),
                         reads=[btm], dma=btm, out=True)
                    P.op('sp', lambda e, s_=s_, src_v=src_v: e.dma_start(out=vn_scr[s_], in_=src_v),
                         reads=[btm], writes=[bVnScr], dma=bVnScr)
                P.op('pool', lambda e: e.dma_start(out=VN4, in_=vn_scr), reads=[bVnScr, bS5, bSC, bACC, bS0],
                     writes=[bKV], dma=bKV)

        def kv_halo():
            P.op('pool', lambda e: e.collective_compute("AllGather", ALU.bypass, replica_groups=[list(range(NCORES))],
                                                        ins=[gin_kv], outs=[gout_kv]),
                 reads=[bGinKV], writes=[bGoutKV])
            GK = HF[:, 0:2048].rearrange("p (j c) -> p j c", j=8)
            GK2 = HF[:, 2048:4096].rearrange("p (j c) -> p j c", j=8)
            HL = HF[:, 4096:4352]
            bh = Buf("halo")
            P.op('sp', lambda e: e.dma_start(out=GK, in_=gout_kv.rearrange("(j p) c -> p j c", p=128)),
                 reads=[bGoutKV] + hz, writes=[bh] + hz, dma=bh)
            P.op('dve', lambda e: e.tensor_tensor(out=GK2, in0=GK, in1=bc_last(SELK[:], 256), op=ALU.mult),
                 reads=[bh, bAtC], writes=[bh])
            P.op('dve', lambda e: e.tensor_reduce(out=HL, in_=GK2.rearrange("p j c -> p c j"), axis=AX.X, op=ALU.add),
                 reads=[bh], writes=[bh])
            P.op('act', lambda e: e.activation(out=VTM[:, 0, :], in_=HL[:, 128:256], func=AF.Copy), reads=[bh],
                 writes=[bKV])
            ps, bps = rPS.next()
            P.op('pe', lambda e: e.transpose(out=ps[:, 0:128], in_=HL[:, 0:128], identity=IDF[:]), reads=[bh, bAtC],
                 writes=[bps])
            P.op('act', lambda e: e.activation(out=KFM[:, 0:128], in_=ps[:, 0:128], func=AF.Copy), reads=[bps],
                 writes=[bKV])

        def attn_A(bl, n_, nb, c_, half):
            q0 = n_ * 128
            h = c_ + 8 * half
            hp = slice(64 * half, 64 * half + 64)
            ps, bps = rPS.next()
            P.op('pe', lambda e: e.matmul(ps[:, 0:256], lhsT=QFM[hp, c_, q0:q0 + 128],
                                          rhs=KFM[hp, nb * 128:nb * 128 + 256], start=True, stop=True),
                 reads=[bQ, bKV], writes=[bps])
            s_, bs_ = rAS.next()
            P.op('dve', lambda e: e.tensor_tensor(out=s_, in0=ps[:, 0:256], in1=BIAS[:, h, :], op=ALU.add),
                 reads=[bps, bBias], writes=[bs_])
            if nb == 0:
                P.op('dve', lambda e: e.tensor_tensor(out=s_, in0=s_, in1=B0M, op=ALU.add), reads=[bBias], writes=[bs_])
            st_, bst = rAST.next()
            P.op('dve', lambda e: e.tensor_reduce(out=st_[:, 0:1], in_=s_, axis=AX.X, op=ALU.max), reads=[bs_],
                 writes=[bst])
            P.op('dve', lambda e: e.tensor_scalar(out=st_[:, 1:2], in0=st_[:, 0:1], scalar1=SNK[:, bl, h:h + 1],
                                                  scalar2=-1.0, op0=ALU.max, op1=ALU.mult),
                 reads=[bst, bAtC], writes=[bst], ss=True)
            e_, be_ = rAE.next()
            P.op('act', lambda e: e.activation(out=e_, in_=s_, func=AF.Exp, bias=st_[:, 1:2], accum_out=st_[:, 2:3]),
                 reads=[bs_, bst], writes=[be_, bst])
            P.op('act', lambda e: e.activation(out=st_[:, 3:4], in_=SNK[:, bl, h:h + 1], func=AF.Exp, bias=st_[:, 1:2]),
                 reads=[bst, bAtC], writes=[bst])
            P.op('dve', lambda e: e.tensor_tensor(out=st_[:, 4:5], in0=st_[:, 2:3], in1=st_[:, 3:4], op=ALU.add),
                 reads=[bst], writes=[bst])
            P.op('dve', lambda e: e.reciprocal(out=st_[:, 5:6], in_=st_[:, 4:5]), reads=[bst], writes=[bst], ss=True)
            pn, bpn = rAPN.next()
            P.op('dve', lambda e: e.tensor_scalar(out=pn, in0=e_, scalar1=st_[:, 5:6], scalar2=None, op0=ALU.mult),
                 reads=[be_, bst], writes=[bpn], ss=True)
            return (n_, nb, c_, half, pn, bpn)

        def attn_B(state, pob):
            n_, nb, c_, half, pn, bpn = state
            q0 = n_ * 128
            hp = slice(64 * half, 64 * half + 64)
            po, bpo = PS[:, pob, :], bPS[pob]
            pt_ps, bpt_ps = rPS.next()
            ptb = pt_ps[:, 0:128].bitcast(BF16)

            def tr(e):
                e.transpose(out=ptb[:, 0:128], in_=pn[:, 0:128], identity=IDB[:])
                return e.transpose(out=ptb[:, 128:256], in_=pn[:, 128:256], identity=IDB[:])
            P.op('pe', tr, reads=[bpn, bAtC], writes=[bpt_ps])
            pt, bpt = rAPT.next()
            P.op('act', lambda e: e.activation(out=pt, in_=ptb, func=AF.Copy), reads=[bpt_ps], writes=[bpt])

            def pv(e):
                e.matmul(po[hp, 0:128], lhsT=VTM[:, nb, hp], rhs=pt[:, 0:128], start=True, stop=False)
                return e.matmul(po[hp, 0:128], lhsT=VTM[:, nb + 1, hp], rhs=pt[:, 128:256], start=False, stop=True)
            P.op('pe', pv, reads=[bpt, bKV], writes=[bpo])
            if half == 1:
                P.op('act', lambda e: e.activation(out=OFM[:, c_, q0:q0 + 128], in_=po[:, 0:128], func=AF.Copy),
                     reads=[bpo], writes=[bO])

        def attn_sample(bl, b, kvh):
            hp = slice(64 * kvh, 64 * kvh + 64)
            ps, bps = rPS.next()
            qs = QS[hp, b, :, :].rearrange("p c s -> p (c s)")

            def mm(e):
                e.matmul(ps[0:32, 0:128], lhsT=qs, rhs=KCF[hp, b, :], start=True, stop=True)
                return e.matmul(ps[0:32, 128:132], lhsT=qs, rhs=KNF[hp, 4 * b:4 * b + 4], start=True, stop=True)
            P.op('pe', mm, reads=[bQ, bKV], writes=[bps])
            rw = [bAT]
            P.op('dve', lambda e: e.tensor_tensor(out=SSs, in0=ps[0:32, 0:132], in1=BSs[:, kvh, :], op=ALU.add),
                 reads=[bps, bBias] + rw, writes=rw)
            P.op('dve', lambda e: e.tensor_reduce(out=SST_[:, 0:1], in_=SSs, axis=AX.X, op=ALU.max), reads=rw,
                 writes=rw, ss=True)
            P.op('dve', lambda e: e.tensor_scalar(out=SST_[:, 1:2], in0=SST_[:, 0:1], scalar1=SNKS[:, bl, kvh:kvh + 1],
                                                  scalar2=-1.0, op0=ALU.max, op1=ALU.mult), reads=rw + [bAtC],
                 writes=rw, ss=True)
            P.op('act', lambda e: e.activation(out=SEs, in_=SSs, func=AF.Exp, bias=SST_[:, 1:2],
                                               accum_out=SST_[:, 2:3]), reads=rw, writes=rw)
            P.op('act', lambda e: e.activation(out=SST_[:, 3:4], in_=SNKS[:, bl, kvh:kvh + 1], func=AF.Exp,
                                               bias=SST_[:, 1:2]), reads=rw + [bAtC], writes=rw)
            P.op('dve', lambda e: e.tensor_tensor(out=SST_[:, 4:5], in0=SST_[:, 2:3], in1=SST_[:, 3:4], op=ALU.add),
                 reads=rw, writes=rw)
            P.op('dve', lambda e: e.reciprocal(out=SST_[:, 5:6], in_=SST_[:, 4:5]), reads=rw, writes=rw, ss=True)
            P.op('dve', lambda e: e.tensor_scalar(out=SPN, in0=SEs, scalar1=SST_[:, 5:6], scalar2=None, op0=ALU.mult),
                 reads=rw, writes=rw, ss=True)
            pt_ps, bpt_ps = rPS.next()
            ptb = pt_ps[:, 0:64].bitcast(BF16)

            def tr(e):
                e.transpose(out=ptb[:, 0:32], in_=SPN[:, 0:128], identity=IDB[0:32, 0:32])
                return e.transpose(out=ptb[0:4, 32:64], in_=SPN[:, 128:132], identity=IDB[0:32, 0:32])
            P.op('pe', tr, reads=rw + [bAtC], writes=[bpt_ps])
            P.op('act', lambda e: e.activation(out=SPT, in_=ptb[:, 0:32], func=AF.Copy), reads=[bpt_ps] + rw, writes=rw)
            P.op('act', lambda e: e.activation(out=SPT4, in_=ptb[0:4, 32:64], func=AF.Copy), reads=[bpt_ps] + rw,
                 writes=rw)
            po, bpo = rPS.next()

            def pv(e):
                e.matmul(po[hp, 0:32], lhsT=VC[:, b, hp], rhs=SPT, start=True, stop=False)
                return e.matmul(po[hp, 0:32], lhsT=VN4[:, b, hp], rhs=SPT4, start=False, stop=True)
            P.op('pe', pv, reads=rw + [bKV], writes=[bpo])
            P.op('act', lambda e: e.activation(out=OFM[hp, :, 512 + 4 * b:512 + 4 * b + 4],
                                               in_=po[hp, 0:32].rearrange("p (c s) -> p c s", c=8), func=AF.Copy),
                 reads=[bpo], writes=[bO])

        def proj_block(wsrc, j, sl, rhs_ap, rhs_buf, evac):
            wb, bwb = rWG.next()
            load_w('pool', wb[:], bwb, wsrc)

            def one(si, off, n, sub):
                oc = 2 * j + sub
                ps, bps = rPS.next()

                def mm(e):
                    ins = None
                    for kc in range(KC):
                        ins = e.matmul(ps[:, :n], lhsT=wb[:, kc, sub * 128:(sub + 1) * 128],
                                       rhs=rhs_ap[:, kc, off:off + n], start=(kc == 0), stop=(kc == KC - 1))
                    return ins
                P.op('pe', mm, reads=[bwb, rhs_buf], writes=[bps])
                evac(si, oc, ps, bps, off, n)
            for si, (ti, lo, hi, off, n) in enumerate(sl):
                for sub in range(2):
                    one(si, off, n, sub)

        def attn_layer(l):
            bl = l - NA
            pobs = [5, 7]
            cnt = 0
            for g in range(len(GROUPS)):
                has_s = (g == len(GROUPS) - 1)
                sl = slots(g)
                prenorm(g, l * 6 + 2)

                def evq(si, oc, ps, bps, off, n):
                    P.op('act', lambda e: e.activation(out=QFM[:, oc, off:off + n], in_=ps[:, :n], func=AF.Identity,
                                                       scale=ATTN_SCALE, bias=BQ8[:, bl, oc:oc + 1]),
                         reads=[bps, bAtC], writes=[bQ])
                for j in range(4):
                    proj_block(wq[bl, j], j, sl, XN, bY, evq)
                work = [(n_, 4 * g + n_, c_, half) for n_ in range(4) for c_ in range(KC) for half in range(2)]
                pend = None
                for (n_, nb, c_, half) in work:
                    st_new = attn_A(bl, n_, nb, c_, half)
                    if pend is not None:
                        attn_B(pend[0], pend[1])
                    pend = (st_new, pobs[(cnt // 2) % 2])
                    cnt += 1
                attn_B(pend[0], pend[1])
                if has_s:
                    P.op('act', lambda e: e.activation(
                        out=QS, in_=QFM[:, :, 512:512 + TS].rearrange("p c (b s) -> p b c s", s=4), func=AF.Copy),
                        reads=[bQ, bAT], writes=[bQ, bAT])
                    for b in range(SB):
                        for kvh in range(2):
                            attn_sample(bl, b, kvh)
                stats = [(PS[:, 6 + si, :], bPS[6 + si]) for si in range(len(sl))]

                def evo(si, oc, ps, bps, off, n):
                    P.op('act', lambda e: e.activation(out=YT[:, oc, off:off + n], in_=ps[:, :n], func=AF.Identity,
                                                       bias=BO[:, bl, oc:oc + 1]), reads=[bps, bAtC], writes=[bYT[si]])
                    stat_accum(si, oc, KC, off, n, stats)
                for j in range(4):
                    proj_block(wo[bl, j], j, sl, OFM, bO, evo)
                postnorm(g, l * 6 + 3, False, stats)

        nl = cfg.get('layers', DEPTH)
        ph = cfg['phases']
        attn_consts()
        for l in range(nl):
            if 'ffn1' in ph:
                for g in range(len(GROUPS)):
                    ffn(l, 0, g)
            if 'mix' in ph:
                if l < NA:
                    s5_layer(l)
                else:
                    attn_layer(l)
            if 'ffn2' in ph:
                for g in range(len(GROUPS)):
                    ffn(l, 1, g)
            if l == NA - 1 and ('mix' in ph or 'kv' in ph):
                kv_phase()

        for ti, (lo, hi) in enumerate(TILES):
            for c in range(KC):
                P.op('sp', (lambda e, c=c, lo=lo, hi=hi: e.dma_start(out=yTv[:, c, lo:hi], in_=X[:, c, lo:hi])),
                     reads=[bX[ti]], dma=bX[ti], out=True)
        P.finish()
        P.replay()
    return nc


def _prep_shared(inp):
    sh = {}
    gu = np.stack([inp['ffn1_w_gu'], inp['ffn2_w_gu']], axis=1).reshape(DEPTH * 2, D, 2 * DFF)
    g_ = gu[:, :, :DFF].reshape(DEPTH * 2, KC, 128, FC, 128)
    u_ = gu[:, :, DFF:].reshape(DEPTH * 2, KC, 128, FC, 128)
    t = np.stack([g_, u_], axis=4)
    sh['wgu'] = np.ascontiguousarray(t.transpose(0, 3, 2, 1, 4, 5)).reshape(DEPTH * 2, FC, 128, KC, 256)
    dn = np.stack([inp['ffn1_w_down'], inp['ffn2_w_down']], axis=1).reshape(DEPTH * 2, FC, 128, KC, 128)
    sh['wdn'] = np.ascontiguousarray(dn.transpose(0, 3, 2, 1, 4))
    g = np.concatenate([inp['norm_g'].reshape(DEPTH * 6, D), inp['kv_norm_g'].reshape(1, D)], axis=0)
    sh['gall'] = np.ascontiguousarray(g.reshape(25, KC, 128).transpose(2, 0, 1))
    lam = np.stack([inp['ssm_lambda_re'], inp['ssm_lambda_im'], inp['ssm_log_dt']], axis=1)
    sh['lamF'] = np.ascontiguousarray(lam.reshape(NA, 3, 4096))
    sh['lamT'] = np.ascontiguousarray(lam.reshape(NA, 3, 32, 128).transpose(0, 3, 1, 2))
    BT = np.zeros((NA, 2, 8, 16, 32, 2, 64), np.float32)
    CP = np.zeros((NA, 2, 2, 64, 32, 8, 16), np.float32)
    for comp, (bk, ck) in enumerate((('ssm_b_re', 'ssm_c_re'), ('ssm_b_im', 'ssm_c_im'))):
        b = inp[bk]
        cc = inp[ck]
        for q in range(32):
            for gl in range(2):
                g8 = 2 * (q % 4) + gl
                BT[:, comp, g8, :, q, gl, :] = b[:, 2 * q + gl].transpose(0, 2, 1)
                CP[:, comp, gl, :, q, g8, :] = cc[:, 2 * q + gl].transpose(0, 2, 1)
    sh['BTd'] = BT.reshape(NA, 2, 128, 4096)
    sh['CPd'] = CP.reshape(NA, 2, 128, 4096)
    sh['dvec'] = np.ascontiguousarray(inp['ssm_d'].reshape(NA, KC, 128).transpose(2, 0, 1))
    sh['bglu'] = np.ascontiguousarray(inp['ssm_b_glu'].reshape(NA, 16, 128).transpose(2, 0, 1))
    wg = inp['ssm_w_glu'].reshape(NA, KC, 128, 2, KC, 128)
    sh['wglu'] = np.ascontiguousarray(wg.transpose(0, 4, 2, 1, 3, 5)).reshape(NA, KC, 128, KC, 256)
    sh['jvec'] = np.ascontiguousarray(np.broadcast_to(np.arange(1, 129, dtype=np.float32), (128, 128)))
    perm = np.array([((cc if pp < 64 else 8 + cc) * 64 + pp % 64) for cc in range(8) for pp in range(128)])
    wqp = inp['attn_w_q'][:, :, perm].reshape(2, KC, 128, 4, 256)
    sh['wq'] = np.ascontiguousarray(wqp.transpose(0, 3, 2, 1, 4))
    wop = inp['attn_w_o'][:, perm, :].reshape(2, KC, 128, 4, 256)
    sh['wo'] = np.ascontiguousarray(wop.transpose(0, 3, 2, 1, 4))
    sh['wkv'] = np.ascontiguousarray(inp['w_kv'].reshape(KC, 128, 256).transpose(1, 0, 2))
    sh['bq'] = np.ascontiguousarray(inp['attn_b_q'][:, perm].reshape(2, KC, 128).transpose(2, 0, 1))
    sh['bo'] = np.ascontiguousarray(inp['attn_b_o'].reshape(2, KC, 128).transpose(2, 0, 1))
    sh['bk'] = np.ascontiguousarray(inp['b_kv'][:128].reshape(128, 1))
    sh['bkv_row'] = np.ascontiguousarray(np.broadcast_to(inp['b_kv'].reshape(1, 256), (128, 256)))
    sh['sinks'] = np.ascontiguousarray(np.broadcast_to(inp['attn_sinks'].reshape(1, 2, 16), (128, 2, 16)))
    ss = np.zeros((32, 2, 2), np.float32)
    for cc in range(8):
        for s_ in range(4):
            for kvh in range(2):
                ss[cc * 4 + s_, :, kvh] = inp['attn_sinks'][:, cc + 8 * kvh]
    sh['sinks_s'] = ss
    sh['ident'] = np.eye(128, dtype=np.float32)

    def bucket(dist):
        n = np.maximum(dist, 0)
        nf = np.maximum(n, 1).astype(np.float32)
        large = 16 + (np.log(nf / np.float32(16)) / np.float32(math.log(128 / 16)) * np.float32(16)).astype(np.int32)
        large = np.minimum(large, 31)
        return np.where(n < 16, n, large)
    rb = inp['rel_bias']
    dist = (np.arange(128)[:, None] + 128) - np.arange(256)[None, :]
    sh['biasT'] = np.ascontiguousarray(rb[bucket(dist)].transpose(0, 2, 1))
    sh['maskT'] = np.where((dist >= 0) & (dist < 128), 0.0, NEG).astype(np.float32)
    dist_s = (np.arange(4)[:, None] + 128) - np.arange(132)[None, :]
    bsg = rb[bucket(dist_s)]
    bs = np.zeros((32, 2, 132), np.float32)
    ms = np.zeros((32, 132), np.float32)
    for cc in range(8):
        for s_ in range(4):
            for kvh in range(2):
                bs[cc * 4 + s_, kvh] = bsg[s_, :, cc + 8 * kvh]
            ms[cc * 4 + s_] = np.where((dist_s[s_] >= 0) & (dist_s[s_] < 128), 0.0, NEG)
    sh['bias_s'] = bs
    sh['mask_s'] = ms
    return sh


def _prep_core(inp, c):
    seq, q = c // 4, c % 4
    xp = inp['x_prompt'][seq, q * TP:(q + 1) * TP]
    xs = inp['x_sample'][c * SB:(c + 1) * SB].reshape(TS, D)
    m = {}
    m['xT'] = np.ascontiguousarray(np.concatenate([xp, xs], axis=0).T)
    st = np.stack([inp['state_ssm_re'][:, c * SB:(c + 1) * SB], inp['state_ssm_im'][:, c * SB:(c + 1) * SB]], axis=1)
    st = st.reshape(NA, 2, SB, 32, 128)
    m['s0d'] = np.ascontiguousarray(st.transpose(0, 4, 1, 3, 2))
    sel = np.zeros((128, 3, 8), np.float32)
    for k in range(1, 4):
        if q - k >= 0:
            sel[:, k - 1, c - k] = 1.0
    m['selS'] = sel
    sk = np.zeros((128, 8), np.float32)
    b0 = np.zeros((128, 256), np.float32)
    if q > 0:
        sk[:, c - 1] = 1.0
    else:
        b0[:, :128] = NEG
    m['selKV'] = sk
    m['blk0mask'] = b0
    m['ck'] = np.ascontiguousarray(inp['cache_win_k'][c * SB:(c + 1) * SB].reshape(SB, 128, 128))
    m['cv'] = np.ascontiguousarray(inp['cache_win_v'][c * SB:(c + 1) * SB].reshape(SB, 128, 128))
    return m


FULL_CFG = {'layers': DEPTH, 'phases': ('ffn1', 'mix', 'ffn2')}


def run(inp, cfg):
    inp = {k: np.asarray(v) for k, v in inp.items()}
    sh = _prep_shared(inp)
    in_maps = []
    for c in range(NCORES):
        m = dict(sh)
        m.update(_prep_core(inp, c))
        in_maps.append(m)
    if not ('ffn1' in cfg['phases'] or 'ffn2' in cfg['phases']):
        for m in in_maps:
            m['wgu'] = m['wgu'][:1]
            m['wdn'] = m['wdn'][:1]
    nc = build(cfg)
    res = run_bass_kernel_spmd(nc, in_maps, core_ids=list(range(NCORES)))
    return res.results


def kernel(**inputs):
    r = run(inputs, FULL_CFG)
    yp = np.zeros((2, 8192, D), np.float32)
    ys = np.zeros((128, 4, D), np.float32)
    re_p = np.zeros((NA, 2, 64, 64), np.float32)
    im_p = np.zeros((NA, 2, 64, 64), np.float32)
    re_s = np.zeros((NA, 128, 64, 64), np.float32)
    im_s = np.zeros((NA, 128, 64, 64), np.float32)
    k_p = np.zeros((2, 128, 2, 64), np.float32)
    v_p = np.zeros((2, 128, 2, 64), np.float32)
    k_s = np.zeros((128, 128, 2, 64), np.float32)
    v_s = np.zeros((128, 128, 2, 64), np.float32)
    for c in range(NCORES):
        seq, q = c // 4, c % 4
        yt = r[c]['yT'].T
        yp[seq, q * TP:(q + 1) * TP] = yt[:TP]
        ys[c * SB:(c + 1) * SB] = yt[TP:].reshape(SB, 4, D)
        sts = r[c]['st_s']
        t = sts.transpose(0, 2, 4, 3, 1).reshape(NA, 2, SB, 64, 64)
        re_s[:, c * SB:(c + 1) * SB] = t[:, 0]
        im_s[:, c * SB:(c + 1) * SB] = t[:, 1]
        k_s[c * SB:(c + 1) * SB] = r[c]['ks_out'].reshape(SB, 128, 2, 64)
        v_s[c * SB:(c + 1) * SB] = r[c]['vs_out'].reshape(SB, 128, 2, 64)
        if q == 3:
            stp = r[c]['st_p']
            t = stp.transpose(0, 2, 3, 1).reshape(NA, 2, 64, 64)
            re_p[:, seq] = t[:, 0]
            im_p[:, seq] = t[:, 1]
            k_p[seq] = r[c]['kv_last'][:, 0:128].reshape(128, 2, 64)
            v_p[seq] = r[c]['kv_last'][:, 128:256].reshape(128, 2, 64)
    return (yp, ys, re_p, im_p, k_p, v_p, re_s, im_s, k_s, v_s)
```

```python
import math
from contextlib import ExitStack
import numpy as np
import concourse.bass as bass
import concourse.mybir as mybir
from concourse.bass_utils import run_bass_kernel_spmd

F32 = mybir.dt.float32
BF16 = mybir.dt.bfloat16
I32 = mybir.dt.int32
AF = mybir.ActivationFunctionType
ALU = mybir.AluOpType
AX = mybir.AxisListType

NCORES = 8
D = 1024
DFF = 2816
DEPTH = 4
NA = 2
KC = 8
FC = 22
TP = 2048
TS = 64
T = TP + TS
SB = 16
TILES = [(0, 512), (512, 1024), (1024, 1536), (1536, 1792), (1792, 2112)]
GROUPS = [[0], [1], [2], [3, 4]]
GMAX = 576
EPS = 1e-6
NEG = -30000.0
ATTN_SCALE = 0.125
ENGS = ('pe', 'act', 'dve', 'pool', 'sp')


class Buf:
    __slots__ = ('name', 'w', 'r')

    def __init__(self, name):
        self.name = name
        self.w = None
        self.r = {}


class Prog:
    def __init__(self, nc, stack):
        self.nc = nc
        self.stack = stack
        self.q = {k: [] for k in ENGS}
        self.cnt = {}
        self.semh = {}
        self.known = {k: {} for k in ENGS}
        self.out_toks = []
        for k in ENGS:
            self._sem('e_' + k)
        self.ndsem = 0
        self.dsem_of = {}

    def _sem(self, key):
        if key not in self.semh:
            self.semh[key] = self.stack.enter_context(self.nc.semaphore(key))
            self.cnt[key] = 0
        return key

    def dsem(self, buf):
        k = self.dsem_of.get(id(buf))
        if k is None:
            k = self._sem('d%d' % self.ndsem)
            self.ndsem += 1
            self.dsem_of[id(buf)] = k
        return k

    def op(self, eng, fn, reads=(), writes=(), dma=None, out=False, ss=False):
        deps = {}

        def add(tok):
            if tok is not None and deps.get(tok[0], 0) < tok[1]:
                deps[tok[0]] = tok[1]
        for b in reads:
            add(b.w)
        for b in writes:
            add(b.w)
            for s, v in b.r.items():
                add((s, v))
        own = 'e_' + eng
        kn = self.known[eng]
        waits = []
        for s, v in deps.items():
            if dma is None and s == own:
                if not (ss and self.cnt[own] - v <= 1):
                    continue
            if kn.get(s, 0) >= v:
                continue
            kn[s] = v
            waits.append((s, v))
        if dma is None:
            s, inc = own, 1
        else:
            s, inc = self.dsem(dma), 16
        self.cnt[s] += inc
        tok = (s, self.cnt[s])
        for b in reads:
            if b.r.get(s, 0) < tok[1]:
                b.r[s] = tok[1]
        for b in writes:
            b.w = tok
            b.r = {}
        self.q[eng].append((waits, fn, s, inc))
        if out:
            self.out_toks.append(tok)
        return tok

    def finish(self):
        final = {}
        for s, v in self.out_toks:
            final[s] = max(final.get(s, 0), v)
        for k in ENGS:
            if k != 'sp':
                final['e_' + k] = self.cnt['e_' + k]
        waits = [(s, v) for s, v in final.items() if v > 0]
        self.q['sp'].append((waits, None, None, 0))

    def replay(self):
        nc = self.nc
        semh = self.semh
        q = self.q
        with nc.Block() as block:
            def mk(name):
                def body(e):
                    for waits, fn, s, inc in q[name]:
                        for ws, wv in waits:
                            e.wait_ge(semh[ws], wv)
                        if fn is not None:
                            ins = fn(e)
                            ins.then_inc(semh[s], inc)
                return body
            block.tensor(mk('pe'))
            block.scalar(mk('act'))
            block.vector(mk('dve'))
            block.gpsimd(mk('pool'))
            block.sync(mk('sp'))


class Rot:
    def __init__(self, items):
        self.items = items
        self.i = 0

    def next(self):
        it = self.items[self.i % len(self.items)]
        self.i += 1
        return it


def build(cfg):
    nc = bass.Bass("TRN2", target_bir_lowering=False)

    def din(name, shape, dt=F32):
        return nc.dram_tensor(name, list(shape), dt, kind="ExternalInput").ap()

    def dout(name, shape, dt=F32):
        return nc.dram_tensor(name, list(shape), dt, kind="ExternalOutput").ap()

    def dint(name, shape, dt=F32):
        return nc.dram_tensor(name, list(shape), dt, kind="Internal").ap()

    xT = din("xT", [D, T])
    nlw = DEPTH * 2 if ('ffn1' in cfg['phases'] or 'ffn2' in cfg['phases']) else 1
    wgu = din("wgu", [nlw, FC, 128, KC, 256])
    wdn = din("wdn", [nlw, KC, 128, FC, 128])
    gall = din("gall", [128, 25, KC])
    yT = dout("yT", [D, T])
    lamT = din("lamT", [NA, 128, 3, 32])
    lamF = din("lamF", [NA, 3, 4096])
    BTd = din("BTd", [NA, 2, 128, 4096])
    CPd = din("CPd", [NA, 2, 128, 4096])
    dvec = din("dvec", [128, NA, KC])
    bglu = din("bglu", [128, NA, 16])
    wglu = din("wglu", [NA, KC, 128, KC, 256])
    s0d = din("s0d", [NA, 128, 2, 32, SB])
    selS = din("selS", [128, 3, 8])
    jvec = din("jvec", [128, 128])
    st_p = dout("st_p", [NA, 128, 2, 32])
    st_s = dout("st_s", [NA, 128, 2, 32, SB])
    W13d = dint("W13d", [NA, KC, 128, 16, 128], BF16)
    wgu_s = dint("wgu_s", [DEPTH * 2, FC, 128, KC, 256], BF16)
    wdn_s = dint("wdn_s", [DEPTH * 2, KC, 128, FC, 128], BF16)
    wq = din("wq", [2, 4, 128, KC, 256])
    wo = din("wo", [2, 4, 128, KC, 256])
    wkv = din("wkv", [128, KC, 256])
    bq = din("bq", [128, 2, KC])
    bo = din("bo", [128, 2, KC])
    bk = din("bk", [128, 1])
    bkv_row = din("bkv_row", [128, 256])
    sinks = din("sinks", [128, 2, 16])
    sinks_s = din("sinks_s", [32, 2, 2])
    selKV = din("selKV", [128, 8])
    ident = din("ident", [128, 128])
    biasT = din("biasT", [128, 16, 256])
    maskT = din("maskT", [128, 256])
    bias_s = din("bias_s", [32, 2, 132])
    mask_s = din("mask_s", [32, 132])
    blk0mask = din("blk0mask", [128, 256])
    ck = din("ck", [SB, 128, 128])
    cv = din("cv", [SB, 128, 128])
    kv_last = dout("kv_last", [128, 256])
    ks_out = dout("ks_out", [SB, 128, 128])
    vs_out = dout("vs_out", [SB, 128, 128])
    gin_kv = dint("gin_kv", [128, 256])
    gout_kv = dint("gout_kv", [NCORES * 128, 256])
    vn_scr = dint("vn_scr", [4, SB, 128])
    dbg_sv = dout("dbg_sv", [4, 128, 1024])
    dbg_ta = dout("dbg_ta", [4, 128, 8192])
    dbg_w = dout("dbg_w", [KC, 128, 16, 128], BF16)
    gin = [dint("gin%d" % l, [128, 64]) for l in range(NA)]
    gout = [dint("gout%d" % l, [NCORES * 128, 64]) for l in range(NA)]

    with ExitStack() as stack:
        P = Prog(nc, stack)

        def sb(name, shape, dt=F32):
            return stack.enter_context(nc.sbuf_tensor(name, list(shape), dt))

        def bc_last(ap, n):
            return bass.AP(ap.tensor, ap.offset, [list(x) for x in ap.ap] + [[0, n]])

        def bc_col(ap, n):
            return bass.AP(ap.tensor, ap.offset, [list(ap.ap[0]), [0, n]])

        def bc_mid(ap, m):
            a = [list(x) for x in ap.ap]
            return bass.AP(ap.tensor, ap.offset, [a[0], [0, m]] + a[1:])

        X = sb("X", [128, KC, T])
        YTb = sb("YTb", [128, KC * GMAX])
        YT = YTb[:].rearrange("p (c t) -> p c t", c=KC)
        XN = YTb[:, 0:KC * GMAX // 2].bitcast(BF16).rearrange("p (c t) -> p c t", c=KC)
        HB = sb("HB", [128, FC * GMAX], BF16)
        H = HB[:].rearrange("p (f t) -> p f t", f=FC)
        HF = HB[:].bitcast(F32)
        HI = HB[:].bitcast(I32)
        WG = [sb("WG%d" % i, [128, KC, 256], BF16) for i in range(3)]
        WD = [sb("WD%d" % i, [128, FC, 128], BF16) for i in range(2)]
        G = sb("G", [128, 25, KC])
        GH = sb("GH", [128, 25, KC])
        SQ = [sb("SQ%d" % i, [128, 512], BF16) for i in range(3)]
        RS = [sb("RS%d" % i, [128, 512]) for i in range(2)]
        SG = [sb("SG%d" % i, [128, 512]) for i in range(2)]
        TMP = [sb("TMP%d" % i, [128, 512]) for i in range(2)]
        ONES = sb("ONES", [128, 128], BF16)
        EPSC = sb("EPSC", [128, 1])
        PS = stack.enter_context(nc.psum_tensor("PS", [128, 8, 512], F32))
        PERS = sb("PERS", [128, 9216])
        MIX = sb("MIX", [128, 4224])
        DV = sb("DV", [128, NA, KC])
        DG = sb("DG", [128, NA, KC])
        BGL = sb("BGL", [128, NA, 16])
        SELS = sb("SELS", [128, 3, 8])
        ITMP = sb("ITMP", [128, 512], I32)
        ITS = sb("ITS", [128, 32], I32)
        TA = PERS[:, 0:4096].rearrange("p (q j) -> p q j", q=32)
        TB = PERS[:, 4096:8192].rearrange("p (q j) -> p q j", q=32)
        RS2 = PERS[:, 8192:8192 + GMAX]
        SV = MIX[:, 0:1024].rearrange("p (k q) -> p k q", k=32)
        SC = [MIX[:, 1024 + 128 * i:1024 + 128 * (i + 1)] for i in range(8)]
        XF = [MIX[:, 2048 + 256 * i:2048 + 256 * (i + 1)].rearrange("p (c j) -> p c j", c=2) for i in range(2)]
        ACC = MIX[:, 2560:3072].rearrange("p (k b q) -> p k b q", k=4, b=4)
        S0 = MIX[:, 3072:4096].rearrange("p (c q b) -> p c q b", c=2, q=32)
        JV = MIX[:, 4096:4224]
        YGB = H[:, 0:8, :]
        XSL = [H[:, 8 + 2 * i:10 + 2 * i, :] for i in range(6)]

        NT = len(TILES)
        bX = [Buf("X%d" % i) for i in range(NT)]
        bY = Buf("YTXN")
        bXN = [bY, bY]
        bYT = [bY, bY]
        bH = [Buf("H%d" % i) for i in range(2)]
        rWG = Rot([(WG[i], Buf("WG%d" % i)) for i in range(3)])
        rWD = Rot([(WD[i], Buf("WD%d" % i)) for i in range(2)])
        bPS = [Buf("PS%d" % i) for i in range(8)]
        rPS = Rot([(PS[:, i, :], bPS[i]) for i in range(4)])
        rSQ = Rot([(SQ[i], Buf("SQ%d" % i)) for i in range(3)])
        rRS = Rot([(RS[i], Buf("RS%d" % i)) for i in range(2)])
        rSG = Rot([(SG[i], Buf("SG%d" % i)) for i in range(2)])
        rTMP = Rot([(TMP[i], Buf("TMP%d" % i)) for i in range(2)])
        bG = Buf("G")
        bC = Buf("consts")
        bS5 = Buf("s5small")
        bTab = Buf("tables")
        bRS2 = Buf("RS2")
        bYGB = Buf("YGB")
        rXS = Rot([(XSL[i], Buf("XS%d" % i)) for i in range(6)])
        rXF = Rot([(XF[i], Buf("XF%d" % i)) for i in range(2)])
        bSC = Buf("SC")
        bACC = Buf("ACC")
        bS0 = Buf("S0")
        bW13 = [Buf("W13d%d" % l) for l in range(NA)]
        bGin = [Buf("gin%d" % l) for l in range(NA)]
        bGout = [Buf("gout%d" % l) for l in range(NA)]

        def slots(g):
            res = []
            off = 0
            for ti in GROUPS[g]:
                lo, hi = TILES[ti]
                res.append((ti, lo, hi, off, hi - lo))
                off += hi - lo
            return res

        P.op('dve', lambda e: e.memset(ONES[:], 1.0), writes=[bC])
        P.op('dve', lambda e: e.memset(EPSC[:], EPS), writes=[bC])
        P.op('sp', lambda e: e.dma_start(out=G[:], in_=gall), writes=[bG], dma=bG)
        P.op('dve', lambda e: e.tensor_scalar(out=GH[:], in0=G[:], scalar1=0.5, scalar2=None, op0=ALU.mult),
             reads=[bG], writes=[bC])
        bSm = Buf("smallin")
        for dst, src in ((DV, dvec), (BGL, bglu), (SELS, selS)):
            P.op('sp', (lambda e, dst=dst, src=src: e.dma_start(out=dst[:], in_=src)), writes=[bSm], dma=bSm)
        xTv = xT.rearrange("(c p) t -> p c t", p=128)
        yTv = yT.rearrange("(c p) t -> p c t", p=128)
        for ti, (lo, hi) in enumerate(TILES):
            for c in range(KC):
                P.op('sp', (lambda e, c=c, lo=lo, hi=hi: e.dma_start(out=X[:, c, lo:hi], in_=xTv[:, c, lo:hi])),
                     writes=[bX[ti]], dma=bX[ti])

        def finish_stats(ps, bps, n):
            rs, brs = rRS.next()
            P.op('act', (lambda e: e.activation(out=rs[:, :n], in_=ps[:, :n], func=AF.Sqrt,
                                                bias=EPSC[:], scale=1.0 / D)),
                 reads=[bps, bC], writes=[brs])
            P.op('dve', (lambda e: e.reciprocal(out=rs[:, :n], in_=rs[:, :n])), reads=[brs], writes=[brs])
            return rs, brs

        def rms_stats(src_fn, n, src_bufs):
            ps, bps = rPS.next()
            for c in range(KC):
                sq, bsq = rSQ.next()
                P.op('act', (lambda e, c=c, sq=sq: e.activation(out=sq[:, :n], in_=src_fn(c), func=AF.Square)),
                     reads=src_bufs, writes=[bsq])
                P.op('pe', (lambda e, c=c, sq=sq, ps=ps: e.matmul(ps[:, :n], lhsT=ONES[:], rhs=sq[:, :n],
                                                                 start=(c == 0), stop=(c == KC - 1))),
                     reads=[bsq, bC], writes=[bps])
            return finish_stats(ps, bps, n)

        def prenorm(g, gidx, keep_rs=False):
            for si, (ti, lo, hi, off, n) in enumerate(slots(g)):
                rs, brs = rms_stats(lambda c, lo=lo, hi=hi: X[:, c, lo:hi], n, [bX[ti]])
                for c in range(KC):
                    P.op('dve', (lambda e, c=c, lo=lo, hi=hi, off=off, n=n, rs=rs: e.scalar_tensor_tensor(
                        out=XN[:, c, off:off + n], in0=X[:, c, lo:hi], scalar=G[:, gidx, c:c + 1],
                        in1=rs[:, :n], op0=ALU.mult, op1=ALU.mult)),
                        reads=[bX[ti], brs, bG], writes=[bXN[si]])
                if keep_rs:
                    P.op('act', (lambda e, off=off, n=n, rs=rs: e.activation(out=RS2[:, off:off + n], in_=rs[:, :n],
                                                                            func=AF.Copy)),
                         reads=[brs], writes=[bRS2])

        def postnorm(g, gidx, half, stats):
            GG = GH if half else G
            for si, (ti, lo, hi, off, n) in enumerate(slots(g)):
                ps, bps = stats[si]
                rs, brs = finish_stats(ps, bps, n)
                for c in range(KC):
                    tmp, btmp = rTMP.next()
                    P.op('dve', (lambda e, c=c, off=off, n=n, rs=rs, tmp=tmp: e.scalar_tensor_tensor(
                        out=tmp[:, :n], in0=YT[:, c, off:off + n], scalar=GG[:, gidx, c:c + 1],
                        in1=rs[:, :n], op0=ALU.mult, op1=ALU.mult)),
                        reads=[bYT[si], brs, bG, bC], writes=[btmp])
                    P.op('dve', (lambda e, c=c, lo=lo, hi=hi, n=n, tmp=tmp: e.tensor_tensor(
                        out=X[:, c, lo:hi], in0=X[:, c, lo:hi], in1=tmp[:, :n], op=ALU.add)),
                        reads=[btmp], writes=[bX[ti]])

        def load_w(eng, dst, bdst, src, extra_reads=()):
            P.op(eng, (lambda e: e.dma_start(out=dst, in_=src, max_dma_last_dim=4096)),
                 reads=list(extra_reads), writes=[bdst], dma=bdst)

        def stat_accum(si, oc, noc, off, n, stats):
            sq, bsq = rSQ.next()
            P.op('act', (lambda e: e.activation(out=sq[:, :n], in_=YT[:, oc, off:off + n], func=AF.Square)),
                 reads=[bYT[si]], writes=[bsq])
            sps, bsps = stats[si]
            P.op('pe', (lambda e: e.matmul(sps[:, :n], lhsT=ONES[:], rhs=sq[:, :n], start=(oc == 0),
                                           stop=(oc == noc - 1))),
                 reads=[bsq, bC], writes=[bsps])

        bWS = [Buf("wscr%d" % i) for i in range(DEPTH * 2)]

        def load_ffn_w(wb, bwb, src32, scr, li, g):
            if g == 0:
                load_w('pool', wb[:], bwb, src32)
                P.op('sp', (lambda e: e.dma_start(out=scr, in_=wb[:])), reads=[bwb], writes=[bWS[li]], dma=bWS[li])
            else:
                P.op('sp', (lambda e: e.dma_start(out=wb[:], in_=scr)), reads=[bWS[li]], writes=[bwb], dma=bwb)

        def ffn(l, which, g):
            li = l * 2 + which
            prenorm(g, l * 6 + (0 if which == 0 else 4))
            sl = slots(g)
            for j in range(FC):
                wb, bwb = rWG.next()
                load_ffn_w(wb, bwb, wgu[li, j], wgu_s[li, j], li, g)
                for si, (ti, lo, hi, off, n) in enumerate(sl):
                    psg, bpsg = rPS.next()
                    psu, bpsu = rPS.next()

                    def mm(e, wb=wb, psg=psg, psu=psu, off=off, n=n):
                        ins = None
                        for half, ps in ((0, psg), (1, psu)):
                            for kc in range(KC):
                                ins = e.matmul(ps[:, :n], lhsT=wb[:, kc, half * 128:(half + 1) * 128],
                                               rhs=XN[:, kc, off:off + n], start=(kc == 0), stop=(kc == KC - 1))
                        return ins
                    P.op('pe', mm, reads=[bwb, bXN[si]], writes=[bpsg, bpsu])
                    sg, bsg = rSG.next()
                    P.op('act', (lambda e, sg=sg, psg=psg, n=n: e.activation(out=sg[:, :n], in_=psg[:, :n],
                                                                            func=AF.Silu)),
                         reads=[bpsg], writes=[bsg])
                    P.op('dve', (lambda e, sg=sg, psu=psu, j=j, off=off, n=n: e.tensor_tensor(
                        out=H[:, j, off:off + n], in0=sg[:, :n], in1=psu[:, :n], op=ALU.mult)),
                        reads=[bsg, bpsu], writes=[bH[si]])
            stats = [(PS[:, 6 + si, :], bPS[6 + si]) for si in range(len(sl))]
            for oc in range(KC):
                wb, bwb = rWD.next()
                load_ffn_w(wb, bwb, wdn[li, oc], wdn_s[li, oc], li, g)
                for si, (ti, lo, hi, off, n) in enumerate(sl):
                    ps, bps = rPS.next()

                    def mm(e, wb=wb, ps=ps, off=off, n=n):
                        ins = None
                        for fc in range(FC):
                            ins = e.matmul(ps[:, :n], lhsT=wb[:, fc, :], rhs=H[:, fc, off:off + n],
                                           start=(fc == 0), stop=(fc == FC - 1))
                        return ins
                    P.op('pe', mm, reads=[bwb, bH[si]], writes=[bps])
                    P.op('act', (lambda e, oc=oc, ps=ps, off=off, n=n: e.activation(
                        out=YT[:, oc, off:off + n], in_=ps[:, :n], func=AF.Copy)),
                        reads=[bps], writes=[bYT[si]])
                    stat_accum(si, oc, KC, off, n, stats)
            postnorm(g, l * 6 + (1 if which == 0 else 5), True, stats)
        TWO_PI = 2.0 * math.pi
        SIN_SCALE = TWO_PI * (1.0 - 2e-6)
        hz = [bH[0], bH[1]]

        def trig(cyc, sin_out, cos_out, t1, t1i, bufs):
            rw = dict(reads=bufs, writes=bufs, ss=True)
            for _ in range(6):
                P.op('dve', lambda e: e.tensor_scalar(out=t1, in0=cyc, scalar1=0.5, scalar2=None, op0=ALU.is_gt), **rw)
                P.op('dve', lambda e: e.tensor_tensor(out=cyc, in0=cyc, in1=t1, op=ALU.subtract), **rw)
            P.op('act', lambda e: e.activation(out=sin_out, in_=cyc, func=AF.Sin, scale=SIN_SCALE), **rw)
            P.op('dve', lambda e: e.tensor_scalar(out=cyc, in0=cyc, scalar1=0.25, scalar2=None, op0=ALU.add), **rw)
            P.op('dve', lambda e: e.tensor_scalar(out=t1, in0=cyc, scalar1=0.5, scalar2=None, op0=ALU.is_gt), **rw)
            P.op('dve', lambda e: e.tensor_tensor(out=cyc, in0=cyc, in1=t1, op=ALU.subtract), **rw)
            P.op('act', lambda e: e.activation(out=cos_out, in_=cyc, func=AF.Sin, scale=SIN_SCALE), **rw)

        def tt(out, a, b, op, bufs, eng='dve'):
            P.op(eng, lambda e: e.tensor_tensor(out=out, in0=a, in1=b, op=op), reads=bufs, writes=bufs, ss=True)

        def ts(out, a, s1, op0, bufs, s2=None, op1=None):
            if op1 is None:
                P.op('dve', lambda e: e.tensor_scalar(out=out, in0=a, scalar1=s1, scalar2=None, op0=op0),
                     reads=bufs, writes=bufs, ss=True)
            else:
                P.op('dve', lambda e: e.tensor_scalar(out=out, in0=a, scalar1=s1, scalar2=s2, op0=op0, op1=op1),
                     reads=bufs, writes=bufs, ss=True)

        def cmul(ore, oim, are_, aim_, bre_, bim_, t1, t2, bufs):
            tt(t1, are_, bre_, ALU.mult, bufs)
            tt(t2, aim_, bim_, ALU.mult, bufs)
            tt(ore, t1, t2, ALU.subtract, bufs)
            tt(t1, are_, bim_, ALU.mult, bufs)
            tt(t2, aim_, bre_, ALU.mult, bufs)
            tt(oim, t1, t2, ALU.add, bufs)

        LRE, LIM, LDT, DT_, ARE, TH, RDEC = 0, 1, 2, 3, 4, 5, 6
        A1R, A1I, A128R, A128I = 7, 8, 9, 10
        AKR = [11, 13, 15]
        AKI = [12, 14, 16]
        ER, EI, CR, CI = 17, 18, 19, 20
        T0 = 21
        NA1I = 29
        SVI = ITS[:]

        def sv(i):
            return SV[:, i, :]

        UR, UI, PWR, PWI = 27, 28, 30, 31

        def csq(bufs):
            tt(sv(T0), sv(PWR), sv(PWR), ALU.mult, bufs)
            tt(sv(T0 + 1), sv(PWI), sv(PWI), ALU.mult, bufs)
            tt(sv(T0 + 2), sv(PWR), sv(PWI), ALU.mult, bufs)
            tt(sv(PWR), sv(T0), sv(T0 + 1), ALU.subtract, bufs)
            ts(sv(PWI), sv(T0 + 2), 2.0, ALU.mult, bufs)

        def s5_small(l):
            bufs = [bS5, bSC]
            P.op('sp', lambda e: e.dma_start(out=SV[:, 0:3, :], in_=lamT[l]), writes=bufs, dma=bS5)
            P.op('sp', lambda e: e.dma_start(out=JV, in_=jvec), writes=bufs, dma=bS5)
            P.op('sp', lambda e: e.dma_start(out=S0, in_=s0d[l]), writes=[bS0], dma=bS0)
            P.op('act', lambda e: e.activation(out=sv(DT_), in_=sv(LDT), func=AF.Exp), reads=bufs, writes=bufs)
            tt(sv(ARE), sv(LRE), sv(DT_), ALU.mult, bufs)
            tt(sv(TH), sv(LIM), sv(DT_), ALU.mult, bufs)
            P.op('act', lambda e: e.activation(out=sv(RDEC), in_=sv(ARE), func=AF.Exp), reads=bufs, writes=bufs)
            ts(sv(T0), sv(TH), 1.0 / TWO_PI, ALU.mult, bufs)
            trig(sv(T0), sv(UI), sv(UR), sv(T0 + 1), None, bufs)
            tt(sv(A1R), sv(RDEC), sv(UR), ALU.mult, bufs)
            tt(sv(A1I), sv(RDEC), sv(UI), ALU.mult, bufs)
            ts(sv(NA1I), sv(A1I), -1.0, ALU.mult, bufs)
            P.op('dve', lambda e: e.memset(SV[:, ER:EI + 1, :], 0.0), reads=bufs, writes=bufs)
            P.op('dve', lambda e: e.tensor_tensor(out=DG[:, l, :], in0=DV[:, l, :], in1=G[:, l * 6 + 2, :],
                                                  op=ALU.mult), reads=[bSm, bG], writes=[bC])

        def s5_tables(mode):
            bufs = [bS5, bSC, bTab] + hz
            X1 = HF[:, 0:2048].rearrange("p (q j) -> p q j", q=32)
            X2 = HF[:, 2048:4096].rearrange("p (q j) -> p q j", q=32)
            if mode == 'G':
                tt(sv(PWR), sv(A1R), sv(A1R), ALU.max, bufs)
                tt(sv(PWI), sv(A1I), sv(A1I), ALU.max, bufs)
                P.op('dve', lambda e: e.memset(TA[:, :, 127:128], 1.0), reads=bufs, writes=bufs)
                P.op('dve', lambda e: e.memset(TB[:, :, 127:128], 0.0), reads=bufs, writes=bufs)
            else:
                tt(sv(PWR), sv(UR), sv(UR), ALU.max, bufs)
                tt(sv(PWI), sv(UI), sv(UI), ALU.max, bufs)
                P.op('dve', lambda e: e.tensor_copy(out=TA[:, :, 0:1], in_=SV[:, UR, :].unsqueeze(2)),
                     reads=bufs, writes=bufs)
                P.op('dve', lambda e: e.tensor_copy(out=TB[:, :, 0:1], in_=SV[:, UI, :].unsqueeze(2)),
                     reads=bufs, writes=bufs)
            n = 1
            while n < 128:
                if mode == 'G':
                    src = slice(128 - n, 128)
                    dst = slice(128 - 2 * n, 128 - n)
                else:
                    src = slice(0, n)
                    dst = slice(n, 2 * n)
                pr = bc_last(SV[:, PWR, :], n)
                pi = bc_last(SV[:, PWI, :], n)
                x1, x2 = X1[:, :, 0:n], X2[:, :, 0:n]
                tt(x1, TA[:, :, src], pr, ALU.mult, bufs)
                tt(x2, TB[:, :, src], pi, ALU.mult, bufs)
                tt(TA[:, :, dst], x1, x2, ALU.subtract, bufs)
                tt(x1, TA[:, :, src], pi, ALU.mult, bufs)
                tt(x2, TB[:, :, src], pr, ALU.mult, bufs)
                tt(TB[:, :, dst], x1, x2, ALU.add, bufs)
                csq(bufs)
                n *= 2
            if mode == 'G':
                tt(sv(A128R), sv(PWR), sv(PWR), ALU.max, bufs)
                tt(sv(A128I), sv(PWI), sv(PWI), ALU.max, bufs)
                for _ in range(4):
                    csq(bufs)
                tt(sv(AKR[0]), sv(PWR), sv(PWR), ALU.max, bufs)
                tt(sv(AKI[0]), sv(PWI), sv(PWI), ALU.max, bufs)
                csq(bufs)
                tt(sv(AKR[1]), sv(PWR), sv(PWR), ALU.max, bufs)
                tt(sv(AKI[1]), sv(PWI), sv(PWI), ALU.max, bufs)
                cmul(sv(AKR[2]), sv(AKI[2]), sv(AKR[0]), sv(AKI[0]), sv(AKR[1]), sv(AKI[1]), sv(T0), sv(T0 + 1), bufs)

        def s5_weights(l):
            btm = [rTMP.items[0][1], rTMP.items[1][1]]
            bufs = [bS5, bSC, bTab] + hz + [bY] + btm
            F = [HF[:, 512 * i:512 * (i + 1)] for i in range(12)]
            FI = ITMP[:]
            Y = [YTb[:, 512 * i:512 * (i + 1)] for i in range(9)]
            OB = [TMP[0][:].bitcast(BF16), TMP[1][:].bitcast(BF16)]
            lre, lim, ldt, dt_, are_, th_, sn, cs, nr, ni, t1 = F[0:11]
            w1r, w1i, bre, bim, t2, t3, den, cr, ci = Y[0:9]
            for ch in range(KC):
                cols = slice(ch * 512, ch * 512 + 512)
                for k, dst in enumerate((lre, lim, ldt)):
                    src = lamF[l, k, cols]
                    srcb = bass.AP(src.tensor, src.offset, [[0, 128], [1, 512]])
                    P.op('sp', lambda e, dst=dst, srcb=srcb: e.dma_start(out=dst, in_=srcb), writes=bufs, dma=bS5)
                for k, dst in enumerate((bre, bim)):
                    P.op('sp', lambda e, dst=dst, k=k, cols=cols: e.dma_start(out=dst, in_=BTd[l, k, :, cols]),
                         writes=bufs, dma=bS5)
                P.op('act', lambda e: e.activation(out=dt_, in_=ldt, func=AF.Exp), reads=bufs, writes=bufs)
                tt(are_, lre, dt_, ALU.mult, bufs)
                tt(th_, lim, dt_, ALU.mult, bufs)
                ts(th_, th_, 1.0 / TWO_PI, ALU.mult, bufs)
                trig(th_, sn, cs, t1, None, bufs)
                mag = dt_
                P.op('act', lambda e: e.activation(out=mag, in_=are_, func=AF.Exp), reads=bufs, writes=bufs)
                tt(nr, mag, cs, ALU.mult, bufs)
                ts(nr, nr, -1.0, ALU.add, bufs)
                tt(ni, mag, sn, ALU.mult, bufs)
                tt(den, lre, lre, ALU.mult, bufs)
                tt(t1, lim, lim, ALU.mult, bufs)
                tt(den, den, t1, ALU.add, bufs)
                P.op('dve', lambda e: e.reciprocal(out=den, in_=den), reads=bufs, writes=bufs)
                tt(t1, nr, lre, ALU.mult, bufs)
                tt(t2, ni, lim, ALU.mult, bufs)
                tt(cr, t1, t2, ALU.add, bufs)
                tt(cr, cr, den, ALU.mult, bufs)
                tt(t1, ni, lre, ALU.mult, bufs)
                tt(t2, nr, lim, ALU.mult, bufs)
                tt(ci, t1, t2, ALU.subtract, bufs)
                tt(ci, ci, den, ALU.mult, bufs)
                cmul(w1r, w1i, bre, bim, cr, ci, t2, t3, bufs)
                obv = OB[0].rearrange("p (r c j) -> p r c j", r=4, c=2)
                P.op('act', lambda e, obv=obv: e.activation(
                    out=obv[:, :, 0, :], in_=w1r.rearrange("p (r j) -> p r j", r=4), func=AF.Copy),
                    reads=bufs, writes=bufs)
                P.op('act', lambda e, obv=obv: e.activation(
                    out=obv[:, :, 1, :], in_=w1i.rearrange("p (r j) -> p r j", r=4), func=AF.Copy),
                    reads=bufs, writes=bufs)
                P.op('sp', lambda e, ch=ch: e.dma_start(
                    out=W13d[l, ch, :, 0:8, :], in_=OB[0].rearrange("p (k j) -> p k j", k=8)),
                    reads=bufs, writes=[bW13[l]], dma=bW13[l])
                cre, cim = F[0], F[1]
                for k, dst in enumerate((cre, cim)):
                    P.op('sp', lambda e, dst=dst, k=k, cols=cols: e.dma_start(out=dst, in_=CPd[l, k, :, cols]),
                         writes=bufs, dma=bS5)
                obv2 = OB[1].rearrange("p (r c j) -> p r c j", r=4, c=2)
                P.op('act', lambda e, obv2=obv2: e.activation(
                    out=obv2[:, :, 0, :], in_=cre.rearrange("p (r j) -> p r j", r=4), func=AF.Copy),
                    reads=bufs, writes=bufs)
                P.op('act', lambda e, obv2=obv2: e.activation(
                    out=obv2[:, :, 1, :], in_=cim.rearrange("p (r j) -> p r j", r=4), func=AF.Copy, scale=-1.0),
                    reads=bufs, writes=bufs)
                P.op('sp', lambda e, ch=ch: e.dma_start(
                    out=W13d[l, ch, :, 8:16, :], in_=OB[1].rearrange("p (k j) -> p k j", k=8)),
                    reads=bufs, writes=[bW13[l]], dma=bW13[l])
        def bu_matmul(wb, bwb, r, off, n, ch):
            pr, bpr = rPS.next()
            pi, bpi = rPS.next()

            def mm(e):
                e.matmul(pr[:, :n], lhsT=wb[:, r * 2 + 0, :], rhs=XN[:, ch, off:off + n], start=True, stop=True)
                return e.matmul(pi[:, :n], lhsT=wb[:, r * 2 + 1, :], rhs=XN[:, ch, off:off + n],
                                start=True, stop=True)
            P.op('pe', mm, reads=[bwb, bY], writes=[bpr, bpi])
            return pr, bpr, pi, bpi

        SCALL = MIX[:, 1024:2048].rearrange("p (c r j) -> p c r j", c=2, r=4)
        XS4 = [H[:, 8 + 2 * r:10 + 2 * r, :] for r in range(4)]
        XS4W = H[:, 8:16, :]
        bXS4 = Buf("XS4")

        def p1_pair(ch, r, wb, bwb):
            q = ch * 4 + r
            pr, bpr, pi, bpi = bu_matmul(wb, bwb, r, 0, 512, ch)
            for blk in range(4):
                js = slice(blk * 128, blk * 128 + 128)
                for kind, (src, tab) in enumerate(((pr, TA), (pi, TB), (pr, TB), (pi, TA))):
                    P.op('dve', (lambda e, src=src, tab=tab, kind=kind, blk=blk, js=js:
                                 e.scalar_tensor_tensor(out=SC[0], in0=src[:, js], scalar=1.0,
                                                        in1=tab[:, q, :], op0=ALU.mult, op1=ALU.mult,
                                                        accum_out=ACC[:, kind, blk, q:q + 1])),
                         reads=[bpr, bpi, bTab, bSC], writes=[bSC, bACC])

        def load_w13(l, ch):
            wb3, bwb = rWG.next()
            wb = wb3[:].rearrange("p a (b j) -> p (a b) j", b=2)
            load_w('pool', wb, bwb, W13d[l, ch], extra_reads=[bW13[l]])
            return wb, bwb

        def s5_pass1(l):
            s5_tables('G')
            for g in range(len(GROUPS)):
                prenorm(g, l * 6 + 2)
                P.op('dve', lambda e: e.memset(ACC[:], 0.0), reads=[bACC], writes=[bACC])
                for ch in range(KC):
                    wb, bwb = load_w13(l, ch)
                    for r in range(4):
                        p1_pair(ch, r, wb, bwb)
                bufs = [bS5, bSC, bACC]
                for blk in range(4):
                    tt(sv(T0), ACC[:, 0, blk, :], ACC[:, 1, blk, :], ALU.subtract, bufs)
                    tt(sv(T0 + 1), ACC[:, 2, blk, :], ACC[:, 3, blk, :], ALU.add, bufs)
                    cmul(sv(T0 + 2), sv(T0 + 3), sv(A128R), sv(A128I), sv(ER), sv(EI), sv(T0 + 4), sv(T0 + 5), bufs)
                    tt(sv(ER), sv(T0 + 2), sv(T0), ALU.add, bufs)
                    tt(sv(EI), sv(T0 + 3), sv(T0 + 1), ALU.add, bufs)

        def s5_exchange(l):
            bufs = [bS5, bSC]
            btm = [rTMP.items[0][1], rTMP.items[1][1]]
            EG = TMP[0][:].rearrange("p (j c) -> p j c", j=8)
            EGT = TMP[1][:].rearrange("p (j c) -> p j c", j=8)
            P.op('sp', lambda e: e.dma_start(out=gin[l].rearrange("p (c q) -> p c q", c=2), in_=SV[:, ER:EI + 1, :]),
                 reads=bufs, writes=[bGin[l]], dma=bGin[l])
            P.op('pool', lambda e: e.collective_compute("AllGather", ALU.bypass,
                                                        replica_groups=[list(range(NCORES))],
                                                        ins=[gin[l]], outs=[gout[l]]),
                 reads=[bGin[l]], writes=[bGout[l]])
            P.op('sp', lambda e: e.dma_start(out=EG, in_=gout[l].rearrange("(j p) c -> p j c", p=128)),
                 reads=[bGout[l]], writes=[btm[0]], dma=btm[0])
            allb = bufs + btm + [bSm]
            P.op('dve', lambda e: e.memset(SV[:, CR:CI + 1, :], 0.0), reads=allb, writes=allb)
            for k in range(3):
                tt(EGT, EG, bc_last(SELS[:, k, :], 64), ALU.mult, allb)
                P.op('dve', lambda e: e.tensor_reduce(out=SV[:, T0:T0 + 2, :].rearrange("p a q -> p (a q)"),
                                                      in_=EGT.rearrange("p j c -> p c j"), axis=AX.X, op=ALU.add),
                     reads=allb, writes=allb, ss=True)
                if k == 0:
                    tt(sv(CR), sv(CR), sv(T0), ALU.add, allb)
                    tt(sv(CI), sv(CI), sv(T0 + 1), ALU.add, allb)
                else:
                    cmul(sv(T0 + 2), sv(T0 + 3), sv(AKR[k - 1]), sv(AKI[k - 1]), sv(T0), sv(T0 + 1), sv(T0 + 4),
                         sv(T0 + 5), allb)
                    tt(sv(CR), sv(CR), sv(T0 + 2), ALU.add, allb)
                    tt(sv(CI), sv(CI), sv(T0 + 3), ALU.add, allb)
            cmul(sv(T0 + 2), sv(T0 + 3), sv(AKR[0]), sv(AKI[0]), sv(CR), sv(CI), sv(T0 + 4), sv(T0 + 5), allb)
            tt(sv(ER), sv(ER), sv(T0 + 2), ALU.add, allb)
            tt(sv(EI), sv(EI), sv(T0 + 3), ALU.add, allb)
            P.op('sp', lambda e: e.dma_start(out=st_p[l], in_=SV[:, ER:EI + 1, :]), reads=allb, dma=bS5, out=True)

        def p2_block(q, blk, pr, bpr, pi, bpi, xs, bxs):
            js = slice(blk * 128, blk * 128 + 128)
            ct, st = TA[:, q, :], TB[:, q, :]
            rd = [bpr, bpi, bTab, bSC, bS5]
            ta_, tc_ = SC[0], SC[2]
            m_re, m_im = SC[4], SC[5]
            pA, pB, z_re, z_im = (PS[:, 6, 128 * i:128 * (i + 1)] for i in range(4))
            pC, pD = PS[:, 7, 0:128], PS[:, 7, 128:256]
            rd = rd + [bPS[6], bPS[7]]
            wr_ = (bSC, bPS[6], bPS[7])

            def o(fn, reads=rd, writes=wr_):
                P.op('dve', fn, reads=reads, writes=list(writes))
            o(lambda e: e.tensor_tensor(out=ta_, in0=pr[:, js], in1=ct, op=ALU.mult))
            o(lambda e: e.tensor_tensor(out=pA, in0=pi[:, js], in1=st, op=ALU.mult))
            o(lambda e: e.tensor_tensor(out=tc_, in0=pi[:, js], in1=ct, op=ALU.mult))
            o(lambda e: e.tensor_tensor(out=pB, in0=pr[:, js], in1=st, op=ALU.mult))
            o(lambda e: e.tensor_tensor(out=m_re, in0=ta_, in1=pA, op=ALU.add))
            o(lambda e: e.tensor_tensor(out=m_im, in0=tc_, in1=pB, op=ALU.subtract))
            rdec = bc_col(SV[:, RDEC, q:q + 1], 128)
            o(lambda e: e.tensor_tensor_scan(out=z_re, data0=rdec, data1=m_re, initial=SV[:, CR, q:q + 1],
                                             op0=ALU.mult, op1=ALU.add))
            o(lambda e: e.tensor_tensor_scan(out=z_im, data0=rdec, data1=m_im, initial=SV[:, CI, q:q + 1],
                                             op0=ALU.mult, op1=ALU.add))
            xf, bxf = rXF.next()
            o(lambda e: e.tensor_tensor(out=ta_, in0=z_re, in1=ct, op=ALU.mult))
            o(lambda e: e.tensor_tensor(out=pC, in0=z_im, in1=st, op=ALU.mult))
            o(lambda e: e.tensor_tensor(out=tc_, in0=z_re, in1=st, op=ALU.mult))
            o(lambda e: e.tensor_tensor(out=pD, in0=z_im, in1=ct, op=ALU.mult))
            o(lambda e: e.tensor_tensor(out=xf[:, 0, :], in0=ta_, in1=pC, op=ALU.subtract), writes=wr_ + (bxf,))
            o(lambda e: e.tensor_tensor(out=xf[:, 1, :], in0=tc_, in1=pD, op=ALU.add), writes=wr_ + (bxf,))
            P.op('dve', lambda e: e.tensor_copy(out=SV[:, CR:CI + 1, q], in_=xf[:, :, 127]), reads=[bxf, bS5],
                 writes=[bS5], ss=True)
            P.op('act', lambda e: e.activation(out=xs[:, :, js], in_=xf[:, :, :], func=AF.Copy),
                 reads=[bxf], writes=[bxs])

        def p2_sample(ch, r, wb, bwb, xs, bxs):
            q = ch * 4 + r
            ps_, bps_ = rPS.next()

            def mm(e):
                e.matmul(ps_[:, 0:TS], lhsT=wb[:, r * 2 + 0, :], rhs=XN[:, ch, 512:512 + TS], start=True, stop=True)
                return e.matmul(ps_[:, TS:2 * TS], lhsT=wb[:, r * 2 + 1, :], rhs=XN[:, ch, 512:512 + TS],
                                start=True, stop=True)
            P.op('pe', mm, reads=[bwb, bY], writes=[bps_])
            bsr = ps_[:, 0:TS].rearrange("p (b s) -> p b s", s=4)
            bsi = ps_[:, TS:2 * TS].rearrange("p (b s) -> p b s", s=4)
            rd = [bps_, bS0, bS5, bSC]
            sr, si_ = S0[:, 0, q, :], S0[:, 1, q, :]
            tq = SC[0][:, 32:48]
            nr_, ni_ = SC[0][:, 0:SB], SC[0][:, SB:2 * SB]
            xsv = xs[:, :, 512:512 + TS].rearrange("p c (b s) -> p c b s", s=4)

            def step(s_):
                P.op('dve', lambda e: e.scalar_tensor_tensor(out=tq, in0=sr, scalar=SV[:, A1R, q:q + 1],
                                                             in1=bsr[:, :, s_], op0=ALU.mult, op1=ALU.add),
                     reads=rd, writes=[bSC], ss=True)
                P.op('dve', lambda e: e.scalar_tensor_tensor(out=nr_, in0=si_, scalar=SV[:, NA1I, q:q + 1], in1=tq,
                                                             op0=ALU.mult, op1=ALU.add), reads=rd, writes=[bSC], ss=True)
                P.op('dve', lambda e: e.scalar_tensor_tensor(out=tq, in0=si_, scalar=SV[:, A1R, q:q + 1],
                                                             in1=bsi[:, :, s_], op0=ALU.mult, op1=ALU.add),
                     reads=rd, writes=[bSC], ss=True)
                P.op('dve', lambda e: e.scalar_tensor_tensor(out=ni_, in0=sr, scalar=SV[:, A1I, q:q + 1], in1=tq,
                                                             op0=ALU.mult, op1=ALU.add), reads=rd, writes=[bSC], ss=True)
                P.op('dve', lambda e: e.tensor_copy(out=S0[:, :, q, :],
                                                    in_=SC[0][:, 0:2 * SB].rearrange("p (c b) -> p c b", c=2)),
                     reads=[bSC], writes=[bS0], ss=True)
                P.op('act', lambda e: e.activation(out=xsv[:, :, :, s_], in_=S0[:, :, q, :], func=AF.Copy),
                     reads=[bS0], writes=[bxs])
            for s_ in range(4):
                step(s_)

        def p2_pair(ch, r, wb, bwb, psy, bpsy, has_s):
            q = ch * 4 + r
            pr, bpr, pi, bpi = bu_matmul(wb, bwb, r, 0, 512, ch)
            xs, bxs = rXS.next()
            for blk in range(4):
                p2_block(q, blk, pr, bpr, pi, bpi, xs, bxs)
            if has_s:
                p2_sample(ch, r, wb, bwb, xs, bxs)

            def mmc(e):
                e.matmul(psy[:, 0:512], lhsT=wb[:, 8 + r * 2, :], rhs=xs[:, 0, 0:512], start=(r == 0), stop=False)
                return e.matmul(psy[:, 0:512], lhsT=wb[:, 8 + r * 2 + 1, :], rhs=xs[:, 1, 0:512],
                                start=False, stop=(r == 3))
            P.op('pe', mmc, reads=[bwb, bxs], writes=[bpsy])
            return xs, bxs

        def p2_yseg(l, g, ch, py, bpy, off, n):
            if off == 0:
                lo = TILES[GROUPS[g][0]][0]
                xb = [bX[ti] for ti in GROUPS[g]]
            else:
                lo = TP
                xb = [bX[GROUPS[g][-1]]]
            tmp, btmp = rTMP.next()
            P.op('dve', lambda e: e.scalar_tensor_tensor(out=tmp[:, :n], in0=X[:, ch, lo:lo + n],
                                                         scalar=DG[:, l, ch:ch + 1], in1=RS2[:, off:off + n],
                                                         op0=ALU.mult, op1=ALU.mult),
                 reads=xb + [bRS2, bC], writes=[btmp])
            P.op('dve', lambda e: e.tensor_tensor(out=tmp[:, :n], in0=tmp[:, :n], in1=py[:, :n], op=ALU.add),
                 reads=[bpy], writes=[btmp])
            P.op('act', lambda e: e.activation(out=YGB[:, ch, off:off + n], in_=tmp[:, :n], func=AF.Gelu_apprx_tanh),
                 reads=[btmp], writes=[bYGB])

        def p2_blk4(l, ch, blk, wb, bwb):
            js = slice(blk * 128, blk * 128 + 128)
            pre, bpre = rPS.next()
            pim, bpim = rPS.next()

            def mm(e):
                ins = None
                for r in range(4):
                    e.matmul(pre[:, r * 128:(r + 1) * 128], lhsT=wb[:, r * 2, :], rhs=XN[:, ch, js], start=True, stop=True)
                    ins = e.matmul(pim[:, r * 128:(r + 1) * 128], lhsT=wb[:, r * 2 + 1, :], rhs=XN[:, ch, js],
                                   start=True, stop=True)
                return ins
            P.op('pe', mm, reads=[bwb, bY], writes=[bpre, bpim])
            (sg0, bsg0), (sg1, bsg1) = rSG.items[0], rSG.items[1]
            PR = pre.rearrange("p (r j) -> p r j", r=4)
            PI = pim.rearrange("p (r j) -> p r j", r=4)
            A0 = sg0[:].rearrange("p (r j) -> p r j", r=4)
            A1 = sg1[:].rearrange("p (r j) -> p r j", r=4)
            CT4 = TA[:, 4 * ch:4 * ch + 4, :]
            ST4 = TB[:, 4 * ch:4 * ch + 4, :]
            MRE, MIM = SCALL[:, 0], SCALL[:, 1]
            ZRE = PS[:, 6, :].rearrange("p (r j) -> p r j", r=4)
            ZIM = PS[:, 7, :].rearrange("p (r j) -> p r j", r=4)
            bb = [bSC, bsg0, bsg1, bPS[6], bPS[7]]
            rd = [bpre, bpim, bTab, bS5] + bb

            def o(fn, reads=rd, writes=bb, ss=False):
                P.op('dve', fn, reads=reads, writes=writes, ss=ss)
            o(lambda e: e.tensor_tensor(out=A0, in0=PR, in1=CT4, op=ALU.mult))
            o(lambda e: e.tensor_tensor(out=A1, in0=PI, in1=ST4, op=ALU.mult))
            o(lambda e: e.tensor_tensor(out=MRE, in0=A0, in1=A1, op=ALU.add))
            o(lambda e: e.tensor_tensor(out=A0, in0=PI, in1=CT4, op=ALU.mult))
            o(lambda e: e.tensor_tensor(out=A1, in0=PR, in1=ST4, op=ALU.mult))
            o(lambda e: e.tensor_tensor(out=MIM, in0=A0, in1=A1, op=ALU.subtract))

            def scan(r, comp):
                q = 4 * ch + r
                rdec = bc_col(SV[:, RDEC, q:q + 1], 128)
                src = SCALL[:, comp, r, :]
                dst = (ZRE if comp == 0 else ZIM)[:, r, :]
                ini = SV[:, CR + comp, q:q + 1]
                o(lambda e: e.tensor_tensor_scan(out=dst, data0=rdec, data1=src, initial=ini, op0=ALU.mult,
                                                 op1=ALU.add))
            for r in range(4):
                scan(r, 0)
                scan(r, 1)
            o(lambda e: e.tensor_tensor(out=A0, in0=ZRE, in1=CT4, op=ALU.mult))
            o(lambda e: e.tensor_tensor(out=A1, in0=ZIM, in1=ST4, op=ALU.mult))
            o(lambda e: e.tensor_tensor(out=MRE, in0=A0, in1=A1, op=ALU.subtract))
            o(lambda e: e.tensor_tensor(out=A0, in0=ZRE, in1=ST4, op=ALU.mult))
            o(lambda e: e.tensor_tensor(out=A1, in0=ZIM, in1=CT4, op=ALU.mult))
            o(lambda e: e.tensor_tensor(out=MIM, in0=A0, in1=A1, op=ALU.add))
            o(lambda e: e.tensor_copy(out=SV[:, CR:CI + 1, 4 * ch:4 * ch + 4], in_=SCALL[:, :, :, 127]),
              writes=bb + [bS5], ss=True)
            P.op('act', lambda e: e.activation(out=XS4W[:, :, js].rearrange("p (r c) j -> p c r j", c=2),
                                               in_=SCALL, func=AF.Copy), reads=[bSC], writes=[bXS4])

        def p1_blk4(ch, blk, wb, bwb):
            js = slice(blk * 128, blk * 128 + 128)
            pre, bpre = rPS.next()
            pim, bpim = rPS.next()

            def mm(e):
                ins = None
                for r in range(4):
                    e.matmul(pre[:, r * 128:(r + 1) * 128], lhsT=wb[:, r * 2, :], rhs=XN[:, ch, js], start=True, stop=True)
                    ins = e.matmul(pim[:, r * 128:(r + 1) * 128], lhsT=wb[:, r * 2 + 1, :], rhs=XN[:, ch, js],
                                   start=True, stop=True)
                return ins
            P.op('pe', mm, reads=[bwb, bY], writes=[bpre, bpim])
            (sg0, bsg0), (sg1, bsg1) = rSG.items[0], rSG.items[1]
            PR = pre.rearrange("p (r j) -> p r j", r=4)
            PI = pim.rearrange("p (r j) -> p r j", r=4)
            A0 = sg0[:].rearrange("p (r j) -> p r j", r=4)
            A1 = sg1[:].rearrange("p (r j) -> p r j", r=4)
            GR4 = TA[:, 4 * ch:4 * ch + 4, :]
            GI4 = TB[:, 4 * ch:4 * ch + 4, :]
            D0, D1 = SCALL[:, 0], SCALL[:, 1]
            bb = [bSC, bsg0, bsg1]
            rd = [bpre, bpim, bTab] + bb

            def o(fn, writes=bb):
                P.op('dve', fn, reads=rd + [bACC], writes=writes)
            o(lambda e: e.tensor_tensor(out=A0, in0=PR, in1=GR4, op=ALU.mult))
            o(lambda e: e.tensor_tensor(out=A1, in0=PI, in1=GI4, op=ALU.mult))
            o(lambda e: e.tensor_tensor(out=D0, in0=A0, in1=A1, op=ALU.subtract))
            o(lambda e: e.tensor_tensor(out=A0, in0=PR, in1=GI4, op=ALU.mult))
            o(lambda e: e.tensor_tensor(out=A1, in0=PI, in1=GR4, op=ALU.mult))
            o(lambda e: e.tensor_tensor(out=D1, in0=A0, in1=A1, op=ALU.add))
            o(lambda e: e.tensor_reduce(out=ACC[:, 0, blk, 4 * ch:4 * ch + 4], in_=D0, axis=AX.X, op=ALU.add),
              writes=bb + [bACC])
            o(lambda e: e.tensor_reduce(out=ACC[:, 2, blk, 4 * ch:4 * ch + 4], in_=D1, axis=AX.X, op=ALU.add),
              writes=bb + [bACC])

        def p2_chunk(l, g, ch, pb, has_s):
            wb, bwb = load_w13(l, ch)
            psy, bpsy = PS[:, pb, :], bPS[pb]
            for blk in range(4):
                p2_blk4(l, ch, blk, wb, bwb)
            if has_s:
                for r in range(4):
                    p2_sample(ch, r, wb, bwb, XS4[r], bXS4)

            def mmc(e):
                ins = None
                for r in range(4):
                    for c in range(2):
                        ins = e.matmul(psy[:, 0:512], lhsT=wb[:, 8 + r * 2 + c, :], rhs=XS4[r][:, c, 0:512],
                                       start=(r == 0 and c == 0), stop=(r == 3 and c == 1))
                return ins
            P.op('pe', mmc, reads=[bwb, bXS4], writes=[bpsy])
            p2_yseg(l, g, ch, psy, bpsy, 0, 512)
            if has_s:
                pys, bpys = rPS.next()

                def mms(e):
                    ins = None
                    for r in range(4):
                        for c in range(2):
                            ins = e.matmul(pys[:, 0:TS], lhsT=wb[:, 8 + r * 2 + c, :], rhs=XS4[r][:, c, 512:512 + TS],
                                           start=(r == 0 and c == 0), stop=(r == 3 and c == 1))
                    return ins
                P.op('pe', mms, reads=[bwb, bXS4], writes=[bpys])
                p2_yseg(l, g, ch, pys, bpys, 512, TS)

        def glu_block(l, j, sl, stats):
            wbg, bwbg = rWG.next()
            load_w('pool', wbg[:], bwbg, wglu[l, j])

            def one(si, off, n):
                psv, bpsv = rPS.next()
                psg, bpsg = rPS.next()

                def mm(e):
                    ins = None
                    for half, ps in ((0, psv), (1, psg)):
                        for kc in range(KC):
                            ins = e.matmul(ps[:, :n], lhsT=wbg[:, kc, half * 128:(half + 1) * 128],
                                           rhs=YGB[:, kc, off:off + n], start=(kc == 0), stop=(kc == KC - 1))
                    return ins
                P.op('pe', mm, reads=[bwbg, bYGB], writes=[bpsv, bpsg])
                sg, bsg = rSG.next()
                P.op('act', lambda e: e.activation(out=sg[:, :n], in_=psg[:, :n], func=AF.Sigmoid,
                                                   bias=BGL[:, l, 8 + j:9 + j]),
                     reads=[bpsg, bSm], writes=[bsg])
                P.op('dve', lambda e: e.scalar_tensor_tensor(out=YT[:, j, off:off + n], in0=psv[:, :n],
                                                             scalar=BGL[:, l, j:j + 1], in1=sg[:, :n],
                                                             op0=ALU.add, op1=ALU.mult),
                     reads=[bpsv, bsg, bSm], writes=[bYT[si]])
                stat_accum(si, j, KC, off, n, stats)
            for si, (ti, lo, hi, off, n) in enumerate(sl):
                one(si, off, n)

        def s5_pass2(l):
            s5_tables('CS')
            psy_banks = [4, 5]
            nchunk = 0
            for g in range(len(GROUPS)):
                has_s = (g == len(GROUPS) - 1)
                prenorm(g, l * 6 + 2, keep_rs=True)
                for ch in range(KC):
                    p2_chunk(l, g, ch, psy_banks[nchunk % 2], has_s)
                    nchunk += 1
                sl = slots(g)
                stats = [(PS[:, 6 + si, :], bPS[6 + si]) for si in range(len(sl))]
                for j in range(KC):
                    glu_block(l, j, sl, stats)
                postnorm(g, l * 6 + 3, False, stats)
            P.op('sp', lambda e: e.dma_start(out=st_s[l], in_=S0), reads=[bS0], dma=bS0, out=True)

        dbgn = [0]

        def dump(tag):
            i = dbgn[0]
            dbgn[0] += 1
            if i >= 4:
                return
            bd = Buf("dbg%d" % i)
            allb = [bS5, bSC, bTab, bACC]
            P.op('sp', lambda e: e.dma_start(out=dbg_sv[i], in_=MIX[:, 0:1024]), reads=allb, dma=bd, out=True)
            P.op('sp', lambda e: e.dma_start(out=dbg_ta[i], in_=PERS[:, 0:8192]), reads=allb, dma=bd, out=True)

        def s5_layer(l):
            stop = cfg.get('stop', 99)
            s5_small(l)
            dump('small')
            s5_weights(l)
            if l == 0:
                bd = Buf("dbgw")
                P.op('sp', lambda e: e.dma_start(out=dbg_w, in_=W13d[0]), reads=[bW13[0]], dma=bd, out=True)
            if stop <= 1:
                return
            s5_pass1(l)
            dump('pass1')
            if stop <= 2:
                return
            s5_exchange(l)
            dump('exch')
            if stop <= 3:
                return
            s5_pass2(l)
            dump('pass2')
        BIAS = PERS[:, 0:4096].rearrange("p (h j) -> p h j", h=16)
        PB = PERS[:, 4096:8320].bitcast(BF16)
        KFM = PB[:, 0:2176]
        VTM = PB[:, 2176:4352].rearrange("p (b c) -> p b c", b=17)
        KCF = PB[:, 4352:6400].rearrange("p (b w) -> p b w", b=SB)
        VC = PB[:, 6400:8448].rearrange("p (b c) -> p b c", b=SB)
        KNF = PERS[:, 8320:8352].bitcast(BF16)
        BSs = PERS[0:32, 8352:8616].rearrange("p (k j) -> p k j", k=2)
        B0M = PERS[:, 8616:8872]
        QFM = H[:, 0:8, :]
        OFM = H[:, 8:16, :]
        bKV = Buf("KV")
        bBias = Buf("bias")
        bQ = Buf("Qfm")
        bO = Buf("Ofm")
        bAT = Buf("attn_tmp")
        AS = [MIX[:, 256 * i:256 * (i + 1)] for i in range(2)]
        AE = [MIX[:, 512 + 256 * i:512 + 256 * (i + 1)] for i in range(2)]
        APN = [MIX[:, 1024 + 128 * i:1024 + 128 * (i + 1)].bitcast(BF16) for i in range(2)]
        APT = [MIX[:, 1280 + 128 * i:1280 + 128 * (i + 1)].bitcast(BF16) for i in range(2)]
        AST = [MIX[:, 1536 + 8 * i:1536 + 8 * (i + 1)] for i in range(4)]
        rAS = Rot([(AS[i], Buf("AS%d" % i)) for i in range(2)])
        rAE = Rot([(AE[i], Buf("AE%d" % i)) for i in range(2)])
        rAPN = Rot([(APN[i], Buf("APN%d" % i)) for i in range(2)])
        rAPT = Rot([(APT[i], Buf("APT%d" % i)) for i in range(2)])
        rAST = Rot([(AST[i], Buf("AST%d" % i)) for i in range(4)])
        VN4 = MIX[0:4, 2048:3072].bitcast(BF16).rearrange("p (b c) -> p b c", b=SB)
        SSs = MIX[0:32, 3072:3204]
        SEs = MIX[0:32, 3204:3336]
        SPN = MIX[0:32, 3336:3402].bitcast(BF16)
        SPT = MIX[:, 3402:3418].bitcast(BF16)
        SPT4 = MIX[0:4, 3418:3434].bitcast(BF16)
        SST_ = MIX[0:32, 3434:3442]
        QS = MIX[:, 3442:3698].bitcast(BF16).rearrange("p (b c s) -> p b c s", b=SB, c=KC)
        IDF = sb("IDF", [128, 128])
        IDB = sb("IDB", [128, 128], BF16)
        BQ8 = sb("BQ8", [128, 2, KC])
        BO = sb("BO", [128, 2, KC])
        BK = sb("BK", [128, 1])
        SNK = sb("SNK", [128, 2, 16])
        SNKS = sb("SNKS", [32, 2, 2])
        SELK = sb("SELK", [128, 8])
        BKVR = sb("BKVR", [128, 256])
        bGinKV = Buf("ginkv")
        bGoutKV = Buf("goutkv")
        bVnScr = Buf("vnscr")
        bAtC = Buf("attn_consts")

        def attn_consts():
            for dst, src in ((IDF, ident), (BQ8, bq), (BO, bo), (BK, bk), (SNK, sinks), (SNKS, sinks_s),
                             (SELK, selKV), (BKVR, bkv_row)):
                P.op('sp', (lambda e, dst=dst, src=src: e.dma_start(out=dst[:], in_=src)), writes=[bAtC], dma=bAtC)
            P.op('dve', lambda e: e.tensor_copy(out=IDB[:], in_=IDF[:]), reads=[bAtC], writes=[bAtC])
            P.op('dve', lambda e: e.tensor_scalar(out=BQ8[:], in0=BQ8[:], scalar1=ATTN_SCALE, scalar2=None,
                                                  op0=ALU.mult), reads=[bAtC], writes=[bAtC])

        def kv_phase():
            bufs_t = [bS5, bSC, bTab, bACC, bS0]
            tm0, btm0 = rTMP.items[0]
            P.op('sp', lambda e: e.dma_start(out=BIAS, in_=biasT), reads=bufs_t, writes=[bBias, bTab], dma=bBias)
            P.op('sp', lambda e: e.dma_start(out=tm0[:, 0:256], in_=maskT), writes=[btm0], dma=btm0)
            P.op('dve', lambda e: e.tensor_tensor(out=BIAS, in0=BIAS, in1=bc_mid(tm0[:, 0:256], 16), op=ALU.add),
                 reads=[bBias, btm0], writes=[bBias])
            P.op('sp', lambda e: e.dma_start(out=BSs, in_=bias_s), reads=bufs_t, writes=[bBias], dma=bBias)
            P.op('sp', lambda e: e.dma_start(out=tm0[0:32, 256:388], in_=mask_s), writes=[btm0], dma=btm0)
            P.op('dve', lambda e: e.tensor_tensor(out=BSs, in0=BSs, in1=bc_mid(tm0[0:32, 256:388], 2), op=ALU.add),
                 reads=[bBias, btm0], writes=[bBias])
            P.op('sp', lambda e: e.dma_start(out=B0M, in_=blk0mask), reads=bufs_t, writes=[bBias], dma=bBias)
            KCN = HB[:, 0:2048].rearrange("p (b c) -> p b c", b=SB)
            bKCN = Buf("KCN")
            P.op('pool', lambda e: e.dma_start(out=KCN, in_=ck.rearrange("b w c -> w b c"), max_dma_last_dim=4096),
                 reads=hz + [bYGB], writes=[bKCN] + hz, dma=bKCN)
            P.op('pool', lambda e: e.dma_start(out=VC, in_=cv.rearrange("b w c -> w b c"), max_dma_last_dim=4096),
                 reads=bufs_t, writes=[bKV], dma=bKV)
            for b in range(SB):
                kv_ktr(b, KCN, bKCN)
            bdd = Buf("dd")
            P.op('sp', lambda e: e.dma_start(out=ks_out[:, 0:124, :], in_=ck[:, 4:128, :]), dma=bdd, out=True)
            P.op('sp', lambda e: e.dma_start(out=vs_out[:, 0:124, :], in_=cv[:, 4:128, :]), dma=bdd, out=True)
            for g in range(len(GROUPS)):
                prenorm(g, 24)
                wb, bwb = rWG.next()
                load_w('pool', wb[:], bwb, wkv)
                for si, (ti, lo, hi, off, n) in enumerate(slots(g)):
                    kv_kfm(wb, bwb, lo, hi, off, n)
                    for t0 in range(lo, hi, 128):
                        kv_tm(wb, bwb, t0, off + (t0 - lo), min(128, hi - t0))
            kv_halo()

        def kv_ktr(b, KCN, bKCN):
            ps, bps = rPS.next()
            psb = ps[:, 0:64].bitcast(BF16)
            P.op('pe', lambda e: e.transpose(out=psb, in_=KCN[:, b, :], identity=IDB[:]), reads=[bKCN, bAtC],
                 writes=[bps])
            P.op('act', lambda e: e.activation(out=KCF[:, b, :], in_=psb, func=AF.Copy), reads=[bps], writes=[bKV])

        def kv_kfm(wb, bwb, lo, hi, off, n):
            ps, bps = rPS.next()

            def mm(e):
                ins = None
                for kc in range(KC):
                    ins = e.matmul(ps[:, :n], lhsT=wb[:, kc, 0:128], rhs=XN[:, kc, off:off + n],
                                   start=(kc == 0), stop=(kc == KC - 1))
                return ins
            P.op('pe', mm, reads=[bwb, bY], writes=[bps])
            npr = min(hi, TP) - lo
            if npr > 0:
                P.op('act', lambda e: e.activation(out=KFM[:, 128 + lo:128 + lo + npr], in_=ps[:, 0:npr],
                                                   func=AF.Identity, bias=BK[:]), reads=[bps, bAtC], writes=[bKV])
            if hi > TP:
                P.op('act', lambda e: e.activation(out=KNF[:, 0:TS], in_=ps[:, npr:npr + TS], func=AF.Identity,
                                                   bias=BK[:]), reads=[bps, bAtC], writes=[bKV])

        def kv_tm(wb, bwb, t0, xoff, nt):
            ps, bps = rPS.next()

            def mm(e):
                ins = None
                for kc in range(KC):
                    ins = e.matmul(ps[0:nt, 0:256], lhsT=XN[:, kc, xoff:xoff + nt], rhs=wb[:, kc, :],
                                   start=(kc == 0), stop=(kc == KC - 1))
                return ins
            P.op('pe', mm, reads=[bwb, bY], writes=[bps])
            tm, btm = rTMP.next()
            P.op('dve', lambda e: e.tensor_tensor(out=tm[0:nt, 0:256], in0=ps[0:nt, 0:256], in1=BKVR[0:nt, :],
                                                  op=ALU.add), reads=[bps, bAtC], writes=[btm])
            if t0 < TP:
                blk = t0 // 128
                P.op('act', lambda e: e.activation(out=VTM[:, blk + 1, :], in_=tm[:, 128:256], func=AF.Copy),
                     reads=[btm], writes=[bKV])
                if blk == 15:
                    P.op('sp', lambda e: e.dma_start(out=kv_last, in_=tm[:, 0:256]), reads=[btm], dma=btm, out=True)
                    P.op('sp', lambda e: e.dma_start(out=gin_kv, in_=tm[:, 0:256]), reads=[btm], writes=[bGinKV],
                         dma=bGinKV)
            else:
                for s_ in range(4):
                    a_k = tm[s_:s_ + 1, 0:128]
                    a_v = tm[s_:s_ + 1, 128:256]
                    pst = tm[:].ap[0][0]
                    src_k = bass.AP(a_k.tensor, a_k.offset, [[4 * pst, SB], [1, 128]])
                    src_v = bass.AP(a_v.tensor, a_v.offset, [[4 * pst, SB], [1, 128]])
                    P.op('sp', lambda e, s_=s_, src_k=src_k: e.dma_start(out=ks_out[:, 124 + s_, :], in_=src_k),
                         reads=[btm], dma=btm, out=True)
                    P.op('sp', lambda e, s_=s_, src_v=src_v: e.dma_start(out=vs_out[:, 124 + s_, :], in_=src_v),
                         reads=[btm], dma=btm, out=True)
                    P.op('sp', lambda e, s_=s_, src_v=src_v: e.dma_start(out=vn_scr[s_], in_=src_v),
                         reads=[btm], writes=[bVnScr], dma=bVnScr)
                P.op('pool', lambda e: e.dma_start(out=VN4, in_=vn_scr), reads=[bVnScr, bS5, bSC, bACC, bS0],
                     writes=[bKV], dma=bKV)

        def kv_halo():
            P.op('pool', lambda e: e.collective_compute("AllGather", ALU.bypass, replica_groups=[list(range(NCORES))],
                                                        ins=[gin_kv], outs=[gout_kv]),
                 reads=[bGinKV], writes=[bGoutKV])
            GK = HF[:, 0:2048].rearrange("p (j c) -> p j c", j=8)
            GK2 = HF[:, 2048:4096].rearrange("p (j c) -> p j c", j=8)
            HL = HF[:, 4096:4352]
            bh = Buf("halo")
            P.op('sp', lambda e: e.dma_start(out=GK, in_=gout_kv.rearrange("(j p) c -> p j c", p=128)),
                 reads=[bGoutKV] + hz, writes=[bh] + hz, dma=bh)
            P.op('dve', lambda e: e.tensor_tensor(out=GK2, in0=GK, in1=bc_last(SELK[:], 256), op=ALU.mult),
                 reads=[bh, bAtC], writes=[bh])
            P.op('dve', lambda e: e.tensor_reduce(out=HL, in_=GK2.rearrange("p j c -> p c j"), axis=AX.X, op=ALU.add),
                 reads=[bh], writes=[bh])
            P.op('act', lambda e: e.activation(out=VTM[:, 0, :], in_=HL[:, 128:256], func=AF.Copy), reads=[bh],
                 writes=[bKV])
            ps, bps = rPS.next()
            P.op('pe', lambda e: e.transpose(out=ps[:, 0:128], in_=HL[:, 0:128], identity=IDF[:]), reads=[bh, bAtC],
                 writes=[bps])
            P.op('act', lambda e: e.activation(out=KFM[:, 0:128], in_=ps[:, 0:128], func=AF.Copy), reads=[bps],
                 writes=[bKV])

        def attn_A(bl, n_, nb, c_, half):
            q0 = n_ * 128
            h = c_ + 8 * half
            hp = slice(64 * half, 64 * half + 64)
            ps, bps = rPS.next()
            P.op('pe', lambda e: e.matmul(ps[:, 0:256], lhsT=QFM[hp, c_, q0:q0 + 128],
                                          rhs=KFM[hp, nb * 128:nb * 128 + 256], start=True, stop=True),
                 reads=[bQ, bKV], writes=[bps])
            s_, bs_ = rAS.next()
            P.op('dve', lambda e: e.tensor_tensor(out=s_, in0=ps[:, 0:256], in1=BIAS[:, h, :], op=ALU.add),
                 reads=[bps, bBias], writes=[bs_])
            if nb == 0:
                P.op('dve', lambda e: e.tensor_tensor(out=s_, in0=s_, in1=B0M, op=ALU.add), reads=[bBias], writes=[bs_])
            st_, bst = rAST.next()
            P.op('dve', lambda e: e.tensor_reduce(out=st_[:, 0:1], in_=s_, axis=AX.X, op=ALU.max), reads=[bs_],
                 writes=[bst])
            P.op('dve', lambda e: e.tensor_scalar(out=st_[:, 1:2], in0=st_[:, 0:1], scalar1=SNK[:, bl, h:h + 1],
                                                  scalar2=-1.0, op0=ALU.max, op1=ALU.mult),
                 reads=[bst, bAtC], writes=[bst], ss=True)
            e_, be_ = rAE.next()
            P.op('act', lambda e: e.activation(out=e_, in_=s_, func=AF.Exp, bias=st_[:, 1:2], accum_out=st_[:, 2:3]),
                 reads=[bs_, bst], writes=[be_, bst])
            P.op('act', lambda e: e.activation(out=st_[:, 3:4], in_=SNK[:, bl, h:h + 1], func=AF.Exp, bias=st_[:, 1:2]),
                 reads=[bst, bAtC], writes=[bst])
            P.op('dve', lambda e: e.tensor_tensor(out=st_[:, 4:5], in0=st_[:, 2:3], in1=st_[:, 3:4], op=ALU.add),
                 reads=[bst], writes=[bst])
            P.op('dve', lambda e: e.reciprocal(out=st_[:, 5:6], in_=st_[:, 4:5]), reads=[bst], writes=[bst], ss=True)
            pn, bpn = rAPN.next()
            P.op('dve', lambda e: e.tensor_scalar(out=pn, in0=e_, scalar1=st_[:, 5:6], scalar2=None, op0=ALU.mult),
                 reads=[be_, bst], writes=[bpn], ss=True)
            return (n_, nb, c_, half, pn, bpn)

        def attn_B(state, pob):
            n_, nb, c_, half, pn, bpn = state
            q0 = n_ * 128
            hp = slice(64 * half, 64 * half + 64)
            po, bpo = PS[:, pob, :], bPS[pob]
            pt_ps, bpt_ps = rPS.next()
            ptb = pt_ps[:, 0:128].bitcast(BF16)

            def tr(e):
                e.transpose(out=ptb[:, 0:128], in_=pn[:, 0:128], identity=IDB[:])
                return e.transpose(out=ptb[:, 128:256], in_=pn[:, 128:256], identity=IDB[:])
            P.op('pe', tr, reads=[bpn, bAtC], writes=[bpt_ps])
            pt, bpt = rAPT.next()
            P.op('act', lambda e: e.activation(out=pt, in_=ptb, func=AF.Copy), reads=[bpt_ps], writes=[bpt])

            def pv(e):
                e.matmul(po[hp, 0:128], lhsT=VTM[:, nb, hp], rhs=pt[:, 0:128], start=True, stop=False)
                return e.matmul(po[hp, 0:128], lhsT=VTM[:, nb + 1, hp], rhs=pt[:, 128:256], start=False, stop=True)
            P.op('pe', pv, reads=[bpt, bKV], writes=[bpo])
            if half == 1:
                P.op('act', lambda e: e.activation(out=OFM[:, c_, q0:q0 + 128], in_=po[:, 0:128], func=AF.Copy),
                     reads=[bpo], writes=[bO])

        def attn_sample(bl, b, kvh):
            hp = slice(64 * kvh, 64 * kvh + 64)
            ps, bps = rPS.next()
            qs = QS[hp, b, :, :].rearrange("p c s -> p (c s)")

            def mm(e):
                e.matmul(ps[0:32, 0:128], lhsT=qs, rhs=KCF[hp, b, :], start=True, stop=True)
                return e.matmul(ps[0:32, 128:132], lhsT=qs, rhs=KNF[hp, 4 * b:4 * b + 4], start=True, stop=True)
            P.op('pe', mm, reads=[bQ, bKV], writes=[bps])
            rw = [bAT]
            P.op('dve', lambda e: e.tensor_tensor(out=SSs, in0=ps[0:32, 0:132], in1=BSs[:, kvh, :], op=ALU.add),
                 reads=[bps, bBias] + rw, writes=rw)
            P.op('dve', lambda e: e.tensor_reduce(out=SST_[:, 0:1], in_=SSs, axis=AX.X, op=ALU.max), reads=rw,
                 writes=rw, ss=True)
            P.op('dve', lambda e: e.tensor_scalar(out=SST_[:, 1:2], in0=SST_[:, 0:1], scalar1=SNKS[:, bl, kvh:kvh + 1],
                                                  scalar2=-1.0, op0=ALU.max, op1=ALU.mult), reads=rw + [bAtC],
                 writes=rw, ss=True)
            P.op('act', lambda e: e.activation(out=SEs, in_=SSs, func=AF.Exp, bias=SST_[:, 1:2],
                                               accum_out=SST_[:, 2:3]), reads=rw, writes=rw)
            P.op('act', lambda e: e.activation(out=SST_[:, 3:4], in_=SNKS[:, bl, kvh:kvh + 1], func=AF.Exp,
                                               bias=SST_[:, 1:2]), reads=rw + [bAtC], writes=rw)
            P.op('dve', lambda e: e.tensor_tensor(out=SST_[:, 4:5], in0=SST_[:, 2:3], in1=SST_[:, 3:4], op=ALU.add),
                 reads=rw, writes=rw)
            P.op('dve', lambda e: e.reciprocal(out=SST_[:, 5:6], in_=SST_[:, 4:5]), reads=rw, writes=rw, ss=True)
            P.op('dve', lambda e: e.tensor_scalar(out=SPN, in0=SEs, scalar1=SST_[:, 5:6], scalar2=None, op0=ALU.mult),
                 reads=rw, writes=rw, ss=True)
            pt_ps, bpt_ps = rPS.next()
            ptb = pt_ps[:, 0:64].bitcast(BF16)

            def tr(e):
                e.transpose(out=ptb[:, 0:32], in_=SPN[:, 0:128], identity=IDB[0:32, 0:32])
                return e.transpose(out=ptb[0:4, 32:64], in_=SPN[:, 128:132], identity=IDB[0:32, 0:32])
            P.op('pe', tr, reads=rw + [bAtC], writes=[bpt_ps])
            P.op('act', lambda e: e.activation(out=SPT, in_=ptb[:, 0:32], func=AF.Copy), reads=[bpt_ps] + rw, writes=rw)
            P.op('act', lambda e: e.activation(out=SPT4, in_=ptb[0:4, 32:64], func=AF.Copy), reads=[bpt_ps] + rw,
                 writes=rw)
            po, bpo = rPS.next()

            def pv(e):
                e.matmul(po[hp, 0:32], lhsT=VC[:, b, hp], rhs=SPT, start=True, stop=False)
                return e.matmul(po[hp, 0:32], lhsT=VN4[:, b, hp], rhs=SPT4, start=False, stop=True)
            P.op('pe', pv, reads=rw + [bKV], writes=[bpo])
            P.op('act', lambda e: e.activation(out=OFM[hp, :, 512 + 4 * b:512 + 4 * b + 4],
                                               in_=po[hp, 0:32].rearrange("p (c s) -> p c s", c=8), func=AF.Copy),
                 reads=[bpo], writes=[bO])

        def proj_block(wsrc, j, sl, rhs_ap, rhs_buf, evac):
            wb, bwb = rWG.next()
            load_w('pool', wb[:], bwb, wsrc)

            def one(si, off, n, sub):
                oc = 2 * j + sub
                ps, bps = rPS.next()

                def mm(e):
                    ins = None
                    for kc in range(KC):
                        ins = e.matmul(ps[:, :n], lhsT=wb[:, kc, sub * 128:(sub + 1) * 128],
                                       rhs=rhs_ap[:, kc, off:off + n], start=(kc == 0), stop=(kc == KC - 1))
                    return ins
                P.op('pe', mm, reads=[bwb, rhs_buf], writes=[bps])
                evac(si, oc, ps, bps, off, n)
            for si, (ti, lo, hi, off, n) in enumerate(sl):
                for sub in range(2):
                    one(si, off, n, sub)

        def attn_layer(l):
            bl = l - NA
            pobs = [4, 5]
            cnt = 0
            for g in range(len(GROUPS)):
                has_s = (g == len(GROUPS) - 1)
                sl = slots(g)
                prenorm(g, l * 6 + 2)

                def evq(si, oc, ps, bps, off, n):
                    P.op('act', lambda e: e.activation(out=QFM[:, oc, off:off + n], in_=ps[:, :n], func=AF.Identity,
                                                       scale=ATTN_SCALE, bias=BQ8[:, bl, oc:oc + 1]),
                         reads=[bps, bAtC], writes=[bQ])
                for j in range(4):
                    proj_block(wq[bl, j], j, sl, XN, bY, evq)
                work = [(n_, 4 * g + n_, c_, half) for n_ in range(4) for c_ in range(KC) for half in range(2)]
                pend = None
                for (n_, nb, c_, half) in work:
                    st_new = attn_A(bl, n_, nb, c_, half)
                    if pend is not None:
                        attn_B(pend[0], pend[1])
                    pend = (st_new, pobs[(cnt // 2) % 2])
                    cnt += 1
                attn_B(pend[0], pend[1])
                if has_s:
                    P.op('act', lambda e: e.activation(
                        out=QS, in_=QFM[:, :, 512:512 + TS].rearrange("p c (b s) -> p b c s", s=4), func=AF.Copy),
                        reads=[bQ, bAT], writes=[bQ, bAT])
                    for b in range(SB):
                        for kvh in range(2):
                            attn_sample(bl, b, kvh)
                stats = [(PS[:, 6 + si, :], bPS[6 + si]) for si in range(len(sl))]

                def evo(si, oc, ps, bps, off, n):
                    P.op('act', lambda e: e.activation(out=YT[:, oc, off:off + n], in_=ps[:, :n], func=AF.Identity,
                                                       bias=BO[:, bl, oc:oc + 1]), reads=[bps, bAtC], writes=[bYT[si]])
                    stat_accum(si, oc, KC, off, n, stats)
                for j in range(4):
                    proj_block(wo[bl, j], j, sl, OFM, bO, evo)
                postnorm(g, l * 6 + 3, False, stats)

        nl = cfg.get('layers', DEPTH)
        ph = cfg['phases']
        attn_consts()
        for l in range(nl):
            if 'ffn1' in ph:
                for g in range(len(GROUPS)):
                    ffn(l, 0, g)
            if 'mix' in ph:
                if l < NA:
                    s5_layer(l)
                else:
                    attn_layer(l)
            if 'ffn2' in ph:
                for g in range(len(GROUPS)):
                    ffn(l, 1, g)
            if l == NA - 1 and ('mix' in ph or 'kv' in ph):
                kv_phase()

        for ti, (lo, hi) in enumerate(TILES):
            for c in range(KC):
                P.op('sp', (lambda e, c=c, lo=lo, hi=hi: e.dma_start(out=yTv[:, c, lo:hi], in_=X[:, c, lo:hi])),
                     reads=[bX[ti]], dma=bX[ti], out=True)
        P.finish()
        P.replay()
    return nc


def _prep_shared(inp):
    sh = {}
    gu = np.stack([inp['ffn1_w_gu'], inp['ffn2_w_gu']], axis=1).reshape(DEPTH * 2, D, 2 * DFF)
    g_ = gu[:, :, :DFF].reshape(DEPTH * 2, KC, 128, FC, 128)
    u_ = gu[:, :, DFF:].reshape(DEPTH * 2, KC, 128, FC, 128)
    t = np.stack([g_, u_], axis=4)
    sh['wgu'] = np.ascontiguousarray(t.transpose(0, 3, 2, 1, 4, 5)).reshape(DEPTH * 2, FC, 128, KC, 256)
    dn = np.stack([inp['ffn1_w_down'], inp['ffn2_w_down']], axis=1).reshape(DEPTH * 2, FC, 128, KC, 128)
    sh['wdn'] = np.ascontiguousarray(dn.transpose(0, 3, 2, 1, 4))
    g = np.concatenate([inp['norm_g'].reshape(DEPTH * 6, D), inp['kv_norm_g'].reshape(1, D)], axis=0)
    sh['gall'] = np.ascontiguousarray(g.reshape(25, KC, 128).transpose(2, 0, 1))
    lam = np.stack([inp['ssm_lambda_re'], inp['ssm_lambda_im'], inp['ssm_log_dt']], axis=1)
    sh['lamF'] = np.ascontiguousarray(lam.reshape(NA, 3, 4096))
    sh['lamT'] = np.ascontiguousarray(lam.reshape(NA, 3, 32, 128).transpose(0, 3, 1, 2))
    BT = np.zeros((NA, 2, 8, 16, 32, 2, 64), np.float32)
    CP = np.zeros((NA, 2, 2, 64, 32, 8, 16), np.float32)
    for comp, (bk, ck) in enumerate((('ssm_b_re', 'ssm_c_re'), ('ssm_b_im', 'ssm_c_im'))):
        b = inp[bk]
        cc = inp[ck]
        for q in range(32):
            for gl in range(2):
                g8 = 2 * (q % 4) + gl
                BT[:, comp, g8, :, q, gl, :] = b[:, 2 * q + gl].transpose(0, 2, 1)
                CP[:, comp, gl, :, q, g8, :] = cc[:, 2 * q + gl].transpose(0, 2, 1)
    sh['BTd'] = BT.reshape(NA, 2, 128, 4096)
    sh['CPd'] = CP.reshape(NA, 2, 128, 4096)
    sh['dvec'] = np.ascontiguousarray(inp['ssm_d'].reshape(NA, KC, 128).transpose(2, 0, 1))
    sh['bglu'] = np.ascontiguousarray(inp['ssm_b_glu'].reshape(NA, 16, 128).transpose(2, 0, 1))
    wg = inp['ssm_w_glu'].reshape(NA, KC, 128, 2, KC, 128)
    sh['wglu'] = np.ascontiguousarray(wg.transpose(0, 4, 2, 1, 3, 5)).reshape(NA, KC, 128, KC, 256)
    sh['jvec'] = np.ascontiguousarray(np.broadcast_to(np.arange(1, 129, dtype=np.float32), (128, 128)))
    perm = np.array([((cc if pp < 64 else 8 + cc) * 64 + pp % 64) for cc in range(8) for pp in range(128)])
    wqp = inp['attn_w_q'][:, :, perm].reshape(2, KC, 128, 4, 256)
    sh['wq'] = np.ascontiguousarray(wqp.transpose(0, 3, 2, 1, 4))
    wop = inp['attn_w_o'][:, perm, :].reshape(2, KC, 128, 4, 256)
    sh['wo'] = np.ascontiguousarray(wop.transpose(0, 3, 2, 1, 4))
    sh['wkv'] = np.ascontiguousarray(inp['w_kv'].reshape(KC, 128, 256).transpose(1, 0, 2))
    sh['bq'] = np.ascontiguousarray(inp['attn_b_q'][:, perm].reshape(2, KC, 128).transpose(2, 0, 1))
    sh['bo'] = np.ascontiguousarray(inp['attn_b_o'].reshape(2, KC, 128).transpose(2, 0, 1))
    sh['bk'] = np.ascontiguousarray(inp['b_kv'][:128].reshape(128, 1))
    sh['bkv_row'] = np.ascontiguousarray(np.broadcast_to(inp['b_kv'].reshape(1, 256), (128, 256)))
    sh['sinks'] = np.ascontiguousarray(np.broadcast_to(inp['attn_sinks'].reshape(1, 2, 16), (128, 2, 16)))
    ss = np.zeros((32, 2, 2), np.float32)
    for cc in range(8):
        for s_ in range(4):
            for kvh in range(2):
                ss[cc * 4 + s_, :, kvh] = inp['attn_sinks'][:, cc + 8 * kvh]
    sh['sinks_s'] = ss
    sh['ident'] = np.eye(128, dtype=np.float32)

    def bucket(dist):
        n = np.maximum(dist, 0)
        nf = np.maximum(n, 1).astype(np.float32)
        large = 16 + (np.log(nf / np.float32(16)) / np.float32(math.log(128 / 16)) * np.float32(16)).astype(np.int32)
        large = np.minimum(large, 31)
        return np.where(n < 16, n, large)
    rb = inp['rel_bias']
    dist = (np.arange(128)[:, None] + 128) - np.arange(256)[None, :]
    sh['biasT'] = np.ascontiguousarray(rb[bucket(dist)].transpose(0, 2, 1))
    sh['maskT'] = np.where((dist >= 0) & (dist < 128), 0.0, NEG).astype(np.float32)
    dist_s = (np.arange(4)[:, None] + 128) - np.arange(132)[None, :]
    bsg = rb[bucket(dist_s)]
    bs = np.zeros((32, 2, 132), np.float32)
    ms = np.zeros((32, 132), np.float32)
    for cc in range(8):
        for s_ in range(4):
            for kvh in range(2):
                bs[cc * 4 + s_, kvh] = bsg[s_, :, cc + 8 * kvh]
            ms[cc * 4 + s_] = np.where((dist_s[s_] >= 0) & (dist_s[s_] < 128), 0.0, NEG)
    sh['bias_s'] = bs
    sh['mask_s'] = ms
    return sh


def _prep_core(inp, c):
    seq, q = c // 4, c % 4
    xp = inp['x_prompt'][seq, q * TP:(q + 1) * TP]
    xs = inp['x_sample'][c * SB:(c + 1) * SB].reshape(TS, D)
    m = {}
    m['xT'] = np.ascontiguousarray(np.concatenate([xp, xs], axis=0).T)
    st = np.stack([inp['state_ssm_re'][:, c * SB:(c + 1) * SB], inp['state_ssm_im'][:, c * SB:(c + 1) * SB]], axis=1)
    st = st.reshape(NA, 2, SB, 32, 128)
    m['s0d'] = np.ascontiguousarray(st.transpose(0, 4, 1, 3, 2))
    sel = np.zeros((128, 3, 8), np.float32)
    for k in range(1, 4):
        if q - k >= 0:
            sel[:, k - 1, c - k] = 1.0
    m['selS'] = sel
    sk = np.zeros((128, 8), np.float32)
    b0 = np.zeros((128, 256), np.float32)
    if q > 0:
        sk[:, c - 1] = 1.0
    else:
        b0[:, :128] = NEG
    m['selKV'] = sk
    m['blk0mask'] = b0
    m['ck'] = np.ascontiguousarray(inp['cache_win_k'][c * SB:(c + 1) * SB].reshape(SB, 128, 128))
    m['cv'] = np.ascontiguousarray(inp['cache_win_v'][c * SB:(c + 1) * SB].reshape(SB, 128, 128))
    return m


FULL_CFG = {'layers': DEPTH, 'phases': ('ffn1', 'mix', 'ffn2')}


def run(inp, cfg):
    inp = {k: np.asarray(v) for k, v in inp.items()}
    sh = _prep_shared(inp)
    in_maps = []
    for c in range(NCORES):
        m = dict(sh)
        m.update(_prep_core(inp, c))
        in_maps.append(m)
    if not ('ffn1' in cfg['phases'] or 'ffn2' in cfg['phases']):
        for m in in_maps:
            m['wgu'] = m['wgu'][:1]
            m['wdn'] = m['wdn'][:1]
    nc = build(cfg)
    res = run_bass_kernel_spmd(nc, in_maps, core_ids=list(range(NCORES)))
    return res.results


def kernel(**inputs):
    r = run(inputs, FULL_CFG)
    yp = np.zeros((2, 8192, D), np.float32)
    ys = np.zeros((128, 4, D), np.float32)
    re_p = np.zeros((NA, 2, 64, 64), np.float32)
    im_p = np.zeros((NA, 2, 64, 64), np.float32)
    re_s = np.zeros((NA, 128, 64, 64), np.float32)
    im_s = np.zeros((NA, 128, 64, 64), np.float32)
    k_p = np.zeros((2, 128, 2, 64), np.float32)
    v_p = np.zeros((2, 128, 2, 64), np.float32)
    k_s = np.zeros((128, 128, 2, 64), np.float32)
    v_s = np.zeros((128, 128, 2, 64), np.float32)
    for c in range(NCORES):
        seq, q = c // 4, c % 4
        yt = r[c]['yT'].T
        yp[seq, q * TP:(q + 1) * TP] = yt[:TP]
        ys[c * SB:(c + 1) * SB] = yt[TP:].reshape(SB, 4, D)
        sts = r[c]['st_s']
        t = sts.transpose(0, 2, 4, 3, 1).reshape(NA, 2, SB, 64, 64)
        re_s[:, c * SB:(c + 1) * SB] = t[:, 0]
        im_s[:, c * SB:(c + 1) * SB] = t[:, 1]
        k_s[c * SB:(c + 1) * SB] = r[c]['ks_out'].reshape(SB, 128, 2, 64)
        v_s[c * SB:(c + 1) * SB] = r[c]['vs_out'].reshape(SB, 128, 2, 64)
        if q == 3:
            stp = r[c]['st_p']
            t = stp.transpose(0, 2, 3, 1).reshape(NA, 2, 64, 64)
            re_p[:, seq] = t[:, 0]
            im_p[:, seq] = t[:, 1]
            k_p[seq] = r[c]['kv_last'][:, 0:128].reshape(128, 2, 64)
            v_p[seq] = r[c]['kv_last'][:, 128:256].reshape(128, 2, 64)
    return (yp, ys, re_p, im_p, k_p, v_p, re_s, im_s, k_s, v_s)
```
